# Optimizing a Trainium2 kernel written in Bass

```python
import jax, jax.numpy as jnp
from jax import lax
import numpy as np

D_MODEL = 2048
BATCH = 2
SEQ = 16384
DEPTH = 1

GRID_W = 64
CTX_LEN = 256
HEAD_DIM = 128
GDN_HEADS = 8
ATTN_HEADS = 8
KV_HEADS = 2
GQA_GROUP = ATTN_HEADS // KV_HEADS
GDN_WIDTH = GDN_HEADS * HEAD_DIM
ATTN_WIDTH = ATTN_HEADS * HEAD_DIM
KV_WIDTH = KV_HEADS * HEAD_DIM
MIX_WIDTH = GDN_WIDTH + ATTN_WIDTH
IN_COLS = 3 * GDN_WIDTH + GDN_WIDTH + 4 * GDN_HEADS + ATTN_WIDTH + 2 * KV_WIDTH + ATTN_WIDTH
CONV_K = 5
CHUNK = 64
Q_BLOCK = 128
ROPE_THETA = 10000.0
ROPE_HALF = HEAD_DIM // 4
EPS = 1e-6

kernel_name = 'hymba_style_gdn_gqa_diffusion_block'


def rms_norm(x, w):
    xf = x.astype(jnp.float32)
    y = xf * lax.rsqrt(jnp.mean(xf * xf, axis=-1, keepdims=True) + EPS)
    return (y * w.astype(jnp.float32)).astype(x.dtype)


def l2_norm(x):
    return x * lax.rsqrt(jnp.sum(x * x, axis=-1, keepdims=True) + EPS)


def split_proj(p):
    sizes = (3 * GDN_WIDTH, GDN_WIDTH, 4 * GDN_HEADS, ATTN_WIDTH, KV_WIDTH, KV_WIDTH, ATTN_WIDTH)
    idx = [int(i) for i in np.cumsum(sizes)[:-1]]
    return jnp.split(p, idx, axis=-1)


def centred_conv(x, w):
    pad = CONV_K // 2
    L = x.shape[1]
    xp = jnp.pad(x, ((0, 0), (pad, pad), (0, 0)))
    out = xp[:, 0:L] * w[0]
    for j in range(1, CONV_K):
        out = out + xp[:, j:j + L] * w[j]
    return out


def axial_rope_tables(L):
    rows_n = L // GRID_W
    row = jnp.repeat(jnp.arange(rows_n, dtype=jnp.int32), GRID_W).astype(jnp.float32)
    col = jnp.tile(jnp.arange(GRID_W, dtype=jnp.int32), rows_n).astype(jnp.float32)
    inv = ROPE_THETA ** (-jnp.arange(ROPE_HALF, dtype=jnp.float32) / ROPE_HALF)
    ang = jnp.stack([row[:, None] * inv, col[:, None] * inv], axis=1)
    return jnp.cos(ang), jnp.sin(ang)


def apply_rope(x, cos, sin):
    B, L, H, _ = x.shape
    xr = x.astype(jnp.float32).reshape(B, L, H, 2, 2, ROPE_HALF)
    x1, x2 = xr[..., 0, :], xr[..., 1, :]
    c = cos[None, :, None]
    s = sin[None, :, None]
    out = jnp.stack([x1 * c - x2 * s, x2 * c + x1 * s], axis=-2)
    return out.reshape(B, L, H, HEAD_DIM).astype(x.dtype)


def chunk_gated_delta(q, k, v, g, beta, s0):
    B, L, H, DK = q.shape
    DV = v.shape[-1]
    n = L // CHUNK

    def chunks(t):
        return t.reshape(B, n, CHUNK, H, -1).transpose(1, 0, 3, 2, 4)

    q_c, k_c, v_c = chunks(q), chunks(k), chunks(v)
    g_c = jnp.cumsum(g.reshape(B, n, CHUNK, H).transpose(1, 0, 3, 2), axis=-1)
    b_c = beta.reshape(B, n, CHUNK, H).transpose(1, 0, 3, 2)[..., None]
    incl = jnp.tril(jnp.ones((CHUNK, CHUNK), dtype=bool))
    strict = jnp.tril(jnp.ones((CHUNK, CHUNK), dtype=bool), -1)
    diff = g_c[..., :, None] - g_c[..., None, :]
    decay = jnp.where(incl, jnp.exp(jnp.where(incl, diff, 0.0)), 0.0)
    k_beta = k_c * b_c
    a = jnp.where(strict, jnp.einsum('nbhid,nbhjd->nbhij', k_beta, k_c) * decay, 0.0)
    eye = jnp.eye(CHUNK, dtype=q.dtype)
    t_inv = lax.linalg.triangular_solve(eye + a, jnp.broadcast_to(eye, a.shape),
                                        left_side=True, lower=True)
    u = jnp.einsum('nbhij,nbhjd->nbhid', t_inv, v_c * b_c)
    w = jnp.einsum('nbhij,nbhjd->nbhid', t_inv, k_beta * jnp.exp(g_c)[..., None])
    intra = jnp.where(incl, jnp.einsum('nbhid,nbhjd->nbhij', q_c, k_c) * decay, 0.0)

    def step(s, inp):
        q_i, k_i, u_i, w_i, g_i, intra_i = inp
        v_new = u_i - jnp.einsum('bhcd,bhde->bhce', w_i, s)
        o_i = (jnp.einsum('bhcd,bhde->bhce', q_i * jnp.exp(g_i)[..., None], s)
               + jnp.einsum('bhij,bhje->bhie', intra_i, v_new))
        g_last = g_i[..., -1:]
        s = (s * jnp.exp(g_last)[..., None]
             + jnp.einsum('bhcd,bhce->bhde', k_i * jnp.exp(g_last - g_i)[..., None], v_new))
        return s, o_i

    s_final, o = lax.scan(step, s0, (q_c, k_c, u, w, g_c, intra))
    return o.transpose(1, 0, 3, 2, 4).reshape(B, L, H, DV), s_final


def gdn_inputs(qkv, gates, conv_w, a_log, dt_bias):
    B, L, _ = qkv.shape
    qkv = jax.nn.silu(centred_conv(qkv, conv_w)).astype(jnp.float32)
    q, k, v = jnp.split(qkv.reshape(B, L, 3, GDN_HEADS, HEAD_DIM), 3, axis=2)
    q = l2_norm(q[:, :, 0]) * (HEAD_DIM ** -0.5)
    k = l2_norm(k[:, :, 0])
    v = v[:, :, 0]
    gates = gates.astype(jnp.float32).reshape(B, L, 2, 2, GDN_HEADS)
    beta = jax.nn.sigmoid(jnp.moveaxis(gates[:, :, 0], 2, 0))
    a_lg = a_log.astype(jnp.float32)[:, None, None, :]
    dtb = dt_bias.astype(jnp.float32)[:, None, None, :]
    g = -jnp.exp(a_lg) * jax.nn.softplus(jnp.moveaxis(gates[:, :, 1], 2, 0) + dtb)
    return q, k, v, beta, g


def flip(t):
    return t[:, ::-1]


def attn_heads(q, k, v, q_norm_w, k_norm_w):
    B, L, _ = q.shape
    q = rms_norm(q.reshape(B, L, ATTN_HEADS, HEAD_DIM), q_norm_w)
    k = rms_norm(k.reshape(B, L, KV_HEADS, HEAD_DIM), k_norm_w)
    v = v.reshape(B, L, KV_HEADS, HEAD_DIM)
    return q, k, v


def attend(q_blk, k_all, v_all):
    s = jnp.einsum('bqkgd,bskd->bkgqs', q_blk, k_all).astype(jnp.float32) * (HEAD_DIM ** -0.5)
    p = jax.nn.softmax(s, axis=-1).astype(v_all.dtype)
    return jnp.einsum('bkgqs,bskd->bqkgd', p, v_all)


def block_attention(q, k_all, v_all):
    B, L, _, _ = q.shape
    nb = L // Q_BLOCK
    qb = q.reshape(B, nb, Q_BLOCK, KV_HEADS, GQA_GROUP, HEAD_DIM).transpose(1, 0, 2, 3, 4, 5)
    o = lax.map(lambda blk: attend(blk, k_all, v_all), qb)
    return o.transpose(1, 0, 2, 3, 4, 5).reshape(B, L, ATTN_WIDTH)


def mixer(h, hc, w_in, conv_w, a_log, dt_bias, gdn_norm_w, q_norm_w, k_norm_w, w_out,
          cos, sin, with_ctx):
    B, L, _ = h.shape
    Lc = hc.shape[1]
    qkv_g, z_g, gates, q_a, k_a, v_a, z_a = split_proj(h @ w_in)
    qkv_gc, z_gc, gates_c, q_ac, k_ac, v_ac, z_ac = split_proj(hc @ w_in)

    ql, kl, vl, bl, gl = gdn_inputs(qkv_g, gates, conv_w, a_log, dt_bias)
    qc, kc, vc, bc, gc = gdn_inputs(qkv_gc, gates_c, conv_w, a_log, dt_bias)
    s0 = jnp.zeros((B, GDN_HEADS, HEAD_DIM, HEAD_DIM), jnp.float32)
    o_cf, s_cf = chunk_gated_delta(qc, kc, vc, gc[0], bc[0], s0)
    o_lf, _ = chunk_gated_delta(ql, kl, vl, gl[0], bl[0], s_cf)
    o_cb, s_cb = chunk_gated_delta(flip(qc), flip(kc), flip(vc), flip(gc[1]), flip(bc[1]), s0)
    o_lb, _ = chunk_gated_delta(flip(ql), flip(kl), flip(vl), flip(gl[1]), flip(bl[1]), s_cb)
    o_gdn = (o_lf + flip(o_lb)).astype(h.dtype)
    gdn_lat = (rms_norm(o_gdn, gdn_norm_w).reshape(B, L, GDN_WIDTH) * jax.nn.silu(z_g))

    qa, ka, va = attn_heads(q_a, k_a, v_a, q_norm_w, k_norm_w)
    qa, ka = apply_rope(qa, cos, sin), apply_rope(ka, cos, sin)
    qac, kac, vac = attn_heads(q_ac, k_ac, v_ac, q_norm_w, k_norm_w)
    k_all = jnp.concatenate([kac, ka], axis=1)
    v_all = jnp.concatenate([vac, va], axis=1)
    attn_lat = block_attention(qa, k_all, v_all) * jax.nn.silu(z_a)

    y = jnp.concatenate([gdn_lat, attn_lat], axis=-1) @ w_out
    if not with_ctx:
        return y, None
    o_gdn_c = (o_cf + flip(o_cb)).astype(hc.dtype)
    gdn_ctx = rms_norm(o_gdn_c, gdn_norm_w).reshape(B, Lc, GDN_WIDTH) * jax.nn.silu(z_gc)
    qcb = qac.reshape(B, Lc, KV_HEADS, GQA_GROUP, HEAD_DIM)
    attn_ctx = attend(qcb, kac, vac).reshape(B, Lc, ATTN_WIDTH) * jax.nn.silu(z_ac)
    yc = jnp.concatenate([gdn_ctx, attn_ctx], axis=-1) @ w_out
    return y, yc


def setup_inputs(seed: int = 0) -> dict:
    key = jax.random.key(seed)
    ks = jax.random.split(key, 16)
    nrm = jax.random.normal
    dt = jnp.exp(jax.random.uniform(ks[10], (DEPTH, 2, GDN_HEADS), minval=np.log(1e-3), maxval=np.log(1e-1)))
    return {
        'x': nrm(ks[0], (BATCH, SEQ, D_MODEL), jnp.float32),
        'c': nrm(ks[1], (BATCH, D_MODEL), jnp.float32),
        'ctx': nrm(ks[2], (BATCH, CTX_LEN, D_MODEL), jnp.float32),
        'c_ctx': nrm(ks[3], (D_MODEL,), jnp.float32),
        'w_ada': nrm(ks[4], (DEPTH, D_MODEL, 3 * D_MODEL), jnp.float32) * D_MODEL ** -0.5,
        'b_ada': nrm(ks[5], (DEPTH, 3 * D_MODEL), jnp.float32) * 0.02,
        'norm_w': 1.0 + 0.02 * nrm(ks[6], (DEPTH, D_MODEL), jnp.float32),
        'w_in': nrm(ks[7], (DEPTH, D_MODEL, IN_COLS), jnp.float32) * D_MODEL ** -0.5,
        'conv_w': nrm(ks[8], (DEPTH, CONV_K, 3 * GDN_WIDTH), jnp.float32) * CONV_K ** -0.5,
        'a_log': jnp.log(jax.random.uniform(ks[9], (DEPTH, 2, GDN_HEADS), minval=1.0, maxval=16.0)),
        'dt_bias': dt + jnp.log(-jnp.expm1(-dt)),
        'gdn_norm_w': 1.0 + 0.02 * nrm(ks[11], (DEPTH, HEAD_DIM), jnp.float32),
        'q_norm_w': 1.0 + 0.02 * nrm(ks[12], (DEPTH, HEAD_DIM), jnp.float32),
        'k_norm_w': 1.0 + 0.02 * nrm(ks[13], (DEPTH, HEAD_DIM), jnp.float32),
        'w_out': nrm(ks[14], (DEPTH, MIX_WIDTH, D_MODEL), jnp.float32) * MIX_WIDTH ** -0.5,
        'final_norm_w': 1.0 + 0.02 * nrm(ks[15], (D_MODEL,), jnp.float32),
    }


def reference(x, c, ctx, c_ctx, w_ada, b_ada, norm_w, w_in, conv_w, a_log, dt_bias,
              gdn_norm_w, q_norm_w, k_norm_w, w_out, final_norm_w):
    cos, sin = axial_rope_tables(x.shape[1])
    for l in range(DEPTH):
        mod = jax.nn.silu(c) @ w_ada[l] + b_ada[l]
        shift, scale, gate = [m[:, None, :] for m in jnp.split(mod, 3, axis=-1)]
        mod_c = jax.nn.silu(c_ctx) @ w_ada[l] + b_ada[l]
        shift_c, scale_c, gate_c = jnp.split(mod_c, 3, axis=-1)
        h = rms_norm(x, norm_w[l]) * (1.0 + scale) + shift
        hc = rms_norm(ctx, norm_w[l]) * (1.0 + scale_c) + shift_c
        with_ctx = l < DEPTH - 1
        y, yc = mixer(h, hc, w_in[l], conv_w[l], a_log[l], dt_bias[l], gdn_norm_w[l],
                      q_norm_w[l], k_norm_w[l], w_out[l], cos, sin, with_ctx)
        x = x + gate * y
        if with_ctx:
            ctx = ctx + gate_c * yc
    return rms_norm(x, final_norm_w)
```

```python
import contextlib
import numpy as np
import concourse.bass as bass
import concourse.mybir as mybir

F32 = mybir.dt.float32
BF16 = mybir.dt.bfloat16
AF = mybir.ActivationFunctionType
ALU = mybir.AluOpType

ENGS = ("pe", "act", "dve", "pool", "sp")


class Buf:
    __slots__ = ("w", "r", "dsem")

    def __init__(self):
        self.w = None
        self.r = []
        self.dsem = None


class Sched:
    def __init__(self, nc, stack):
        self.nc = nc
        self.stack = stack
        self.prog = {e: [] for e in ENGS}
        self.nsem = 0
        self.esem = {e: self._newsem() for e in ENGS if e != "sp"}
        self.ecnt = {e: 0 for e in ENGS}
        self.waited = {e: {} for e in ENGS}
        self.dcnt = {}
        self.bufs = {}
        self.sems = {}
        self.ntens = 0
        self.psum = [nc.alloc_psum_tensor(f"psb{i}", [128, 512], F32) for i in range(6)]
        self.psum_b = [nc.alloc_psum_tensor(f"psh{i}", [128, 1024], BF16) for i in range(2)]

    def _newsem(self):
        self.nsem += 1
        s = self.stack.enter_context(self.nc.semaphore(f"s{self.nsem}"))
        return s

    def sb(self, shape, dtype, name="t"):
        self.ntens += 1
        ps_ = getattr(self, "pstack", None)
        if ps_ is not None:
            return ps_.enter_context(self.nc.sbuf_tensor(f"{name}{self.ntens}", list(shape), dtype))
        return self.nc.alloc_sbuf_tensor(f"{name}{self.ntens}", list(shape), dtype)

    def phase_begin(self):
        self.pstack = contextlib.ExitStack()
        self.phase_bufs = []
        if not hasattr(self, "free_dsems"):
            self.free_dsems = []

    def phase_end(self):
        self.finalize()
        self.prog = {e: [] for e in ENGS}
        for b in self.phase_bufs:
            if b.dsem is not None:
                self.free_dsems.append(b.dsem)
                b.dsem = None
        self.phase_bufs = []
        self.pstack.close()
        self.pstack = None

    def _buf(self, ap):
        n = ap.name
        b = self.bufs.get(n)
        if b is None:
            b = self.bufs[n] = Buf()
            if getattr(self, "pstack", None) is not None:
                self.phase_bufs.append(b)
        return b

    def _is_chip(self, ap):
        return ap.name in self.bufs or not ap.name.startswith("dr_")

    def _deps(self, reads, writes):
        ev = {}

        def add(e):
            if e is None:
                return
            s, v = e
            k = id(s)
            self.sems[k] = s
            if k in self.dcnt:
                v = 16 * self.dcnt[k]
            if ev.get(k, 0) < v:
                ev[k] = v

        for b in reads:
            add(b.w)
        for b in writes:
            add(b.w)
            for e in b.r:
                add(e)
        return ev

    def _waits(self, eng, ev):
        wd = self.waited[eng]
        for k, v in ev.items():
            if eng == "pe" and eng in self.esem and k == id(self.esem[eng]):
                continue
            if wd.get(k, 0) >= v:
                continue
            wd[k] = v
            self.prog[eng].append(("w", self.sems[k], v))

    def _done(self, e, reads, writes):
        for b in writes:
            b.w = e
            b.r = []
        for b in reads:
            b.r.append(e)

    def op(self, eng, fn, outs, ins):
        ins_ = [a for a in ins if a is not None and hasattr(a, "name") and not a.name.startswith("dr_")]
        reads = [self._buf(a) for a in ins_ if not a.name.startswith("ps")]
        writes = [self._buf(a) for a in outs] + [self._buf(a) for a in ins_ if a.name.startswith("ps")]
        self._waits(eng, self._deps(reads, writes))
        if self.ecnt[eng] >= 30000:
            self.esem[eng] = self._newsem()
            self.ecnt[eng] = 0
        self.ecnt[eng] += 1
        s = self.esem[eng]
        self.sems[id(s)] = s
        self.prog[eng].append(("i", fn, s, 1))
        self._done((s, self.ecnt[eng]), reads, writes)

    def dma(self, q, out, in_):
        o_chip = not out.name.startswith("dr_")
        i_chip = not in_.name.startswith("dr_")
        reads = [self._buf(in_)] if i_chip else []
        writes = [self._buf(out)] if o_chip else []
        tb = writes[0] if o_chip else reads[0]
        self._waits(q, self._deps(reads, writes))
        if tb.dsem is None:
            if getattr(self, "free_dsems", None):
                tb.dsem = self.free_dsems.pop()
            else:
                tb.dsem = self._newsem()
                self.dcnt[id(tb.dsem)] = 0
        s = tb.dsem
        k = id(s)
        self.sems[k] = s
        self.dcnt[k] += 1
        self.prog[q].append(("i", lambda e, o=out, i=in_: e.dma_start(out=o, in_=i), s, 16))
        self._done((s, 16 * self.dcnt[k]), reads, writes)

    def barrier(self):
        ev = {}
        for e in self.esem:
            s = self.esem[e]
            self.sems[id(s)] = s
            if self.ecnt[e]:
                ev[id(s)] = self.ecnt[e]
        for k, c in self.dcnt.items():
            if c:
                ev[k] = 16 * c
        for e in ENGS:
            self._waits(e, dict(ev))

    def mm(self, out, lhsT, rhs, start=True, stop=True):
        self.op("pe", lambda e: e.matmul(out, lhsT, rhs, start=start, stop=stop), [out], [lhsT, rhs])

    def tr(self, out, in_, ident):
        self.op("pe", lambda e: e.transpose(out, in_, ident), [out], [in_, ident])

    def act(self, out, in_, func, bias=None, scale=None, accum_out=None):
        kw = {}
        ins = [in_]
        if bias is not None:
            kw["bias"] = bias
            ins.append(bias)
        if scale is not None:
            kw["scale"] = scale
            ins.append(scale)
        outs = [out]
        if accum_out is not None:
            kw["accum_out"] = accum_out
            outs.append(accum_out)
        self.op("act", lambda e: e.activation(out, in_, func, **kw), outs, ins)

    def tt(self, eng, out, in0, in1, op):
        self.op(eng, lambda e: e.tensor_tensor(out, in0, in1, op), [out], [in0, in1])

    def ts(self, eng, out, in0, s1, s2, op0, op1=None):
        if op1 is None:
            self.op(eng, lambda e: e.tensor_scalar(out, in0, s1, None, op0), [out], [in0, s1])
        else:
            self.op(eng, lambda e: e.tensor_scalar(out, in0, s1, s2, op0, op1), [out], [in0, s1, s2])

    def stt(self, eng, out, in0, scalar, in1, op0, op1):
        self.op(eng, lambda e: e.scalar_tensor_tensor(out, in0, scalar, in1, op0, op1), [out], [in0, scalar, in1])

    def copy(self, eng, out, in_):
        if eng == "act":
            self.op("act", lambda e: e.copy(out, in_), [out], [in_])
        else:
            self.op(eng, lambda e: e.tensor_copy(out, in_), [out], [in_])

    def memset(self, eng, out, val):
        self.op(eng, lambda e: e.memset(out, val), [out], [])

    def finalize(self):
        self.barrier()
        nc = self.nc
        prog = self.prog

        def replay(name):
            def f(e):
                for it in prog[name]:
                    if it[0] == "w":
                        e.wait_ge(it[1], it[2])
                    else:
                        it[1](e).then_inc(it[2], it[3])
            return f

        with nc.Block() as block:
            block.sync(replay("sp"))
            block.scalar(replay("act"))
            block.vector(replay("dve"))
            block.gpsimd(replay("pool"))
            block.tensor(replay("pe"))


D = 2048
NKC = 16
HD = 128
EPS = 1e-6
NCOL = 1800
C_Z = 768
C_TM = 1280
C_G = 1792
BIG = 1.0e30


def build_fused(L):
    NT = 2 + L // 128
    NTL = L // 128
    NO = NTL // 4
    LO = NO * 128
    NK = NT * 128
    nc = bass.Bass("TRN2", target_bir_lowering=False)
    dt = nc.dram_tensor

    def ext(name, shape, dtype=F32):
        return dt("dr_" + name, shape, dtype, kind="ExternalInput").ap()

    xh = ext("xh", [NT, 128, D])
    xo = ext("xo", [NO, 128, D])
    cc = ext("cc", [128, 32])
    wada = ext("wada", [NKC, 128, 6144])
    bada = ext("bada", [128, 6144])
    normw = ext("normw", [128, D])
    wc = ext("wc", [4, NKC, 128, NCOL])
    convw = ext("convw", [4, 128, 30])
    qnw = ext("qnw", [128, 128])
    knw = ext("knw", [128, 128])
    rowcs = ext("rowcs", [NTL, 128, 64])
    rowcso = ext("rowcso", [NO, 128, 64])
    colcs = ext("colcs", [128, 64])
    identd = ext("ident", [128, 128])
    alog = ext("alog", [4, 64, 4])
    dtb = ext("dtb", [4, 64, 4])
    gnw = ext("gnw", [128, 1])
    cst = ext("cst", [64, 7 * 64])
    seld = ext("sel", [64, 4])
    wout = ext("wout", [NKC, 128, D])
    fnw = ext("fnw", [128, D])
    out = dt("dr_out", [NO, 128, D], F32, kind="ExternalOutput").ap()
    HTW = dt("dr_HTW", [NT, 128, NKC * 132], BF16).ap()
    HTO = dt("dr_HTO", [NO, 128, NKC * 128], BF16).ap()
    QKT = dt("dr_QKT", [NT, 128, 512], BF16).ap()
    KVT = dt("dr_KVT", [NT, 64, 1024], BF16).ap()
    ZST = dt("dr_ZST", [NO, 128, 512], BF16).ap()
    GTS = dt("dr_GTS", [64, NT * 16], F32).ap()
    QAT = dt("dr_QAT", [2, 128, LO], BF16).ap()
    KAT = dt("dr_KAT", [128, NK], BF16).ap()
    VA = dt("dr_VA", [NT, 128, 128], BF16).ap()
    OD = dt("dr_OD", [2, NTL, 64, 512], F32).ap()
    MIXT = dt("dr_MIXT", [16, 128, LO], BF16).ap()

    with contextlib.ExitStack() as stack:
        S = Sched(nc, stack)
        ps = S.psum
        pb = S.psum_b
        ident_f = S.sb([128, 128], F32, "identf")
        ident_b = S.sb([128, 128], BF16, "identb")
        ones_b = S.sb([128, 128], BF16, "onesb")
        ones_f = S.sb([128, 128], F32, "onesf")
        epsc = S.sb([128, 1], F32, "epsc")
        GT = S.sb([128, D], F32, "GT")
        S.dma("sp", ident_f[:], identd)
        S.copy("dve", ident_b[:], ident_f[:])
        S.memset("pool", ones_b[:], 1.0)
        S.memset("pool", ones_f[:], 1.0)
        S.memset("pool", epsc[:], EPS)

        S.phase_begin()
        ccs = S.sb([128, 32], F32, "ccs")
        S.dma("sp", ccs[:], cc)
        sig = S.sb([128, 32], F32, "sig")
        S.act(sig[:], ccs[:], AF.Sigmoid)
        S.tt("dve", ccs[:], ccs[:], sig[:], ALU.mult)
        G = [S.sb([128, D], F32, "G") for _ in range(2)]
        SH = [S.sb([128, D], F32, "SH") for _ in range(2)]
        nwb = S.sb([128, D], F32, "nwb")
        S.dma("sp", nwb[:], normw)
        scr = S.sb([128, 4096], F32, "scr")
        cB = scr[:, :].rearrange("p (a b) -> p a b", b=128)
        for i in range(32):
            S.ts("dve", cB[:, i, :], ones_f[:], ccs[:, i:i + 1], None, ALU.mult)
        wst = [S.sb([128, 512], F32, "wst") for _ in range(4)]
        bst = [S.sb([128, 512], F32, "bst") for _ in range(2)]
        n_ = 0
        for cb in range(12):
            nv = 2 if cb < 8 else 1
            for kc in range(NKC):
                w_ = wst[n_ % 4]
                n_ += 1
                S.dma("sp" if n_ % 2 else "act", w_[:], wada[kc, :, cb * 512:(cb + 1) * 512])
                for v in range(nv):
                    S.mm(ps[v][:, :], cB[:, v * 16 + kc, :], w_[:], start=(kc == 0), stop=(kc == NKC - 1))
            b_ = bst[cb % 2]
            S.dma("sp", b_[:], bada[:, cb * 512:(cb + 1) * 512])
            for v in range(nv):
                if cb < 4:
                    S.tt("dve", SH[v][:, cb * 512:(cb + 1) * 512], ps[v][:, :], b_[:], ALU.add)
                elif cb < 8:
                    sl = slice((cb - 4) * 512, (cb - 3) * 512)
                    S.tt("dve", G[v][:, sl], ps[v][:, :], b_[:], ALU.add)
                    S.stt("dve", G[v][:, sl], G[v][:, sl], 1.0, nwb[:, sl], ALU.add, ALU.mult)
                else:
                    sl = slice((cb - 8) * 512, (cb - 7) * 512)
                    S.tt("dve", GT[:, sl], ps[0][:, :], b_[:], ALU.add)
        xt = [S.sb([128, D], F32, "xt") for _ in range(2)]
        junk = scr[:, 2048:3072].bitcast(BF16)
        ss = [S.sb([128, 1], F32, "ss") for _ in range(2)]
        t1 = scr[:, 0:2048]
        hb = scr[:, 3072:4096].bitcast(BF16)
        W = [S.sb([128, NKC, 132], BF16, "W") for _ in range(3)]
        ho = [S.sb([128, NKC, 128], BF16, "ho") for _ in range(2)]

        def seq_of(t):
            return 0 if t < 2 else 1

        def normed(src, s_, v):
            S.dma("sp", xt[s_][:], src)
            S.act(junk, xt[s_][:], AF.Square, accum_out=ss[s_][:])
            S.act(ss[s_][:], ss[s_][:], AF.Sqrt, bias=epsc[:, 0:1], scale=1.0 / D)
            S.op("dve", lambda e, o=ss[s_][:]: e.reciprocal(o, o), [ss[s_][:]], [ss[s_][:]])
            S.stt("dve", t1, xt[s_][:], ss[s_][:], G[v][:], ALU.mult, ALU.mult)
            S.tt("pool", hb, t1, SH[v][:], ALU.add)

        for t in range(NT):
            normed(xh[t], t % 2, 0 if t >= 2 else 1)
            Wt = W[t % 3]
            for half in range(2):
                pT = pb[half][:, :].rearrange("p (a b) -> p a b", b=128)
                for k8 in range(8):
                    kc = half * 8 + k8
                    S.tr(pT[:, k8, :], hb[:, kc * 128:(kc + 1) * 128], ident_b[:])
                S.copy("act", Wt[:, half * 8:(half + 1) * 8, 2:130], pT[:, :, :])
                if t > 0 and seq_of(t - 1) == seq_of(t):
                    S.copy("act", W[(t - 1) % 3][:, half * 8:(half + 1) * 8, 130:132], pT[:, :, 0:2])
                if t + 1 < NT and seq_of(t + 1) == seq_of(t):
                    S.copy("act", W[(t + 1) % 3][:, half * 8:(half + 1) * 8, 0:2], pT[:, :, 126:128])
            if t == 0 or seq_of(t - 1) != seq_of(t):
                S.memset("pool", Wt[:, :, 0:2], 0.0)
            if t == NT - 1 or seq_of(t + 1) != seq_of(t):
                S.memset("pool", Wt[:, :, 130:132], 0.0)
            if t >= 1:
                S.dma("act", HTW[t - 1], W[(t - 1) % 3][:].rearrange("p a b -> p (a b)"))
        S.dma("act", HTW[NT - 1], W[(NT - 1) % 3][:].rearrange("p a b -> p (a b)"))
        for i in range(NO):
            normed(xo[i], i % 2, 0)
            for half in range(2):
                pT = pb[half][:, :].rearrange("p (a b) -> p a b", b=128)
                for k8 in range(8):
                    kc = half * 8 + k8
                    S.tr(pT[:, k8, :], hb[:, kc * 128:(kc + 1) * 128], ident_b[:])
                S.copy("act", ho[i % 2][:, half * 8:(half + 1) * 8, :], pT[:, :, :])
            S.dma("act", HTO[i], ho[i % 2][:].rearrange("p a b -> p (a b)"))
        S.phase_end()

        for j in range(4):
            S.phase_begin()
            cw = S.sb([128, 30], F32, "cw")
            S.dma("sp", cw[:], convw[j])
            qw = S.sb([128, 128], F32, "qw")
            kw_ = S.sb([128, 128], F32, "kw")
            S.dma("sp", qw[:], qnw)
            S.dma("sp", kw_[:], knw)
            S.ts("dve", qw[:], qw[:], float(HD ** -0.5), None, ALU.mult)
            cs = [S.sb([128, 2, 2, 32], F32, "cs") for _ in range(2)]
            for s_ in range(2):
                S.dma("sp", cs[s_][:, :, 1, :], colcs.rearrange("p (a f) -> p a f", a=2))
            w_sb = S.sb([128, NKC, NCOL], BF16, "wsb")
            wf = [S.sb([128, NCOL // 2], F32, "wf") for _ in range(2)]
            for kc in range(NKC):
                for hf in range(2):
                    S.dma("sp", wf[hf][:], wc[j, kc, :, hf * 900:(hf + 1) * 900])
                    S.copy("pool" if hf else "dve", w_sb[:, kc, hf * 900:(hf + 1) * 900], wf[hf][:])
            Wl = [S.sb([128, NKC, 132], BF16, "Wl") for _ in range(2)]
            hl = [S.sb([128, NKC, 128], BF16, "hl") for _ in range(2)]
            acc = S.sb([128, 6, 128], F32, "acc")
            slf = S.sb([128, 4, 128], F32, "slf")
            sq = S.sb([128, 512], BF16, "sq")
            rn = S.sb([128, 4, 128], F32, "rn")
            qkT = [S.sb([128, 4, 128], BF16, "qkT") for _ in range(2)]
            vT = S.sb([128, 2, 128], BF16, "vT")
            kv_sb = [S.sb([64, 1024], BF16, "kvsb") for _ in range(2)]
            zs = [S.sb([128, 512], BF16, "zs") for _ in range(2)]
            gts = S.sb([64, NT * 16], F32, "gts")
            ssa = S.sb([128, 4], F32, "ssa")
            junk2 = S.sb([128, 128], BF16, "junk2")
            xn = S.sb([128, 2, 128], F32, "xn")
            xr = S.sb([128, 2, 128], BF16, "xr")
            rt = [S.sb([128, 2, 32], F32, "rt") for _ in range(4)]
            va = [S.sb([128, 128], BF16, "va") for _ in range(2)]
            qkaT = [S.sb([128, 2, 128], BF16, "qkaT") for _ in range(2)]

            def rope(nblk, c_):
                Cc = c_[:, 0, :, :]
                Sn = c_[:, 1, :, :]
                for bl in range(nblk):
                    xv = xn[:, bl, :].rearrange("p (a h f) -> p a h f", a=2, h=2)
                    ov = xr[:, bl, :].rearrange("p (a h f) -> p a h f", a=2, h=2)
                    x1 = xv[:, :, 0, :]
                    x2 = xv[:, :, 1, :]
                    e1 = "dve" if bl % 2 == 0 else "pool"
                    S.tt(e1, rt[0][:], x1, Cc, ALU.mult)
                    S.tt(e1, rt[1][:], x2, Sn, ALU.mult)
                    S.tt(e1, ov[:, :, 0, :], rt[0][:], rt[1][:], ALU.subtract)
                    S.tt(e1, rt[2][:], x2, Cc, ALU.mult)
                    S.tt(e1, rt[3][:], x1, Sn, ALU.mult)
                    S.tt(e1, ov[:, :, 1, :], rt[2][:], rt[3][:], ALU.add)

            for t in range(NT):
                s_ = t % 2
                Wt = Wl[s_]
                lat = t >= 2
                S.dma("sp", Wt[:].rearrange("p a b -> p (a b)"), HTW[t])
                for g in range(2):
                    pg = ps[2 + g][:, 0:396].rearrange("p (a b) -> p a b", b=132)
                    for bl in range(3):
                        blk = g * 3 + bl
                        for kc in range(NKC):
                            S.mm(pg[:, bl, :], w_sb[:, kc, blk * 128:(blk + 1) * 128], Wt[:, kc, :],
                                 start=(kc == 0), stop=(kc == NKC - 1))
                    for bl in range(3):
                        blk = g * 3 + bl
                        S.ts("dve", acc[:, blk, :], pg[:, bl, 0:128], cw[:, blk * 5:blk * 5 + 1], None, ALU.mult)
                        for jj in range(1, 5):
                            S.stt("dve", acc[:, blk, :], pg[:, bl, jj:jj + 128], cw[:, blk * 5 + jj:blk * 5 + jj + 1],
                                  acc[:, blk, :], ALU.mult, ALU.add)
                S.act(slf[:, :, :], acc[:, 0:4, :], AF.Silu)
                S.act(vT[:, :, :], acc[:, 4:6, :], AF.Silu)
                S.act(sq[:], slf[:].rearrange("p a b -> p (a b)"), AF.Square)
                S.mm(ps[4][:, :], ones_b[:], sq[:])
                rnf = rn[:].rearrange("p a b -> p (a b)")
                S.act(rnf, ps[4][:, :], AF.Sqrt, bias=epsc[:, 0:1], scale=1.0)
                S.op("dve", lambda e, o=rnf: e.reciprocal(o, o), [rnf], [rnf])
                S.stt("dve", qkT[s_][:, 0:2, :], slf[:, 0:2, :], float(HD ** -0.5), rn[:, 0:2, :], ALU.mult, ALU.mult)
                S.tt("pool", qkT[s_][:, 2:4, :], slf[:, 2:4, :], rn[:, 2:4, :], ALU.mult)
                S.dma("sp", QKT[t], qkT[s_][:].rearrange("p a b -> p (a b)"))
                pkv = pb[0][0:64, :].rearrange("p (a b) -> p a b", b=128)
                for c in range(2):
                    for h in range(2):
                        S.tr(pkv[:, (c * 2 + h) * 2 + 0, :], qkT[s_][:, 2 + h, c * 64:(c + 1) * 64], ident_b[:])
                        S.tr(pkv[:, (c * 2 + h) * 2 + 1, :], vT[:, h, c * 64:(c + 1) * 64], ident_b[:])
                S.copy("act", kv_sb[s_][:], pkv.rearrange("p a b -> p (a b)"))
                S.dma("act", KVT[t], kv_sb[s_][:])
                pgt = ps[5][0:64, 0:16].rearrange("p (a b) -> p a b", b=8)
                for c in range(2):
                    for kc in range(NKC):
                        S.mm(pgt[:, c, :], Wt[:, kc, 2 + c * 64:2 + (c + 1) * 64], w_sb[:, kc, C_G:C_G + 8],
                             start=(kc == 0), stop=(kc == NKC - 1))
                S.copy("act", gts[:, t * 16:(t + 1) * 16], ps[5][0:64, 0:16])
                if j % 2 == 0:
                    pa = ps[4][:, 0:256].rearrange("p (a b) -> p a b", b=128)
                    for kc in range(NKC):
                        S.mm(ps[4][:, 0:256], Wt[:, kc, 2:130], w_sb[:, kc, C_TM + 256:C_TM + 512], start=(kc == 0), stop=(kc == NKC - 1))
                    S.copy("dve", va[s_][:], pa[:, 1, :])
                    S.dma("act", VA[t], va[s_][:])
                    S.act(junk2[:], pa[:, 0, :], AF.Square, accum_out=ssa[:, 0:1])
                    S.act(ssa[:, 0:1], ssa[:, 0:1], AF.Sqrt, bias=epsc[:, 0:1], scale=1.0 / HD)
                    S.op("dve", lambda e, o=ssa[:, 0:1]: e.reciprocal(o, o), [ssa[:, 0:1]], [ssa[:, 0:1]])
                    S.stt("dve", xn[:, 0, :], pa[:, 0, :], ssa[:, 0:1], kw_[:], ALU.mult, ALU.mult)
                    if lat:
                        c_ = cs[s_]
                        S.dma("sp", c_[:, :, 0, :], rowcs[t - 2].rearrange("p (a f) -> p a f", a=2))
                        rope(1, c_)
                    else:
                        S.copy("pool", xr[:, 0, :], xn[:, 0, :])
                    pq = pb[1][:, 0:128]
                    S.tr(pq, xr[:, 0, :], ident_b[:])
                    S.copy("act", qkaT[s_][:, 0, :], pq)
                    S.dma("sp", KAT[:, t * 128:(t + 1) * 128], qkaT[s_][:, 0, :])
            S.dma("sp", GTS, gts[:])
            for i in range(NO):
                s_ = i % 2
                S.dma("sp", hl[s_][:].rearrange("p a b -> p (a b)"), HTO[i])
                pz = ps[4][:, :].rearrange("p (a b) -> p a b", b=128)
                for bl in range(4):
                    for kc in range(NKC):
                        S.mm(pz[:, bl, :], w_sb[:, kc, C_Z + bl * 128:C_Z + (bl + 1) * 128], hl[s_][:, kc, :],
                             start=(kc == 0), stop=(kc == NKC - 1))
                S.act(zs[s_][:], ps[4][:, :], AF.Silu)
                S.dma("sp", ZST[i], zs[s_][:])
                pa = ps[5][:, 0:256].rearrange("p (a b) -> p a b", b=128)
                for kc in range(NKC):
                    S.mm(ps[5][:, 0:256], hl[s_][:, kc, :], w_sb[:, kc, C_TM:C_TM + 256], start=(kc == 0), stop=(kc == NKC - 1))
                for bl in range(2):
                    S.act(junk2[:], pa[:, bl, :], AF.Square, accum_out=ssa[:, bl:bl + 1])
                S.act(ssa[:, 0:2], ssa[:, 0:2], AF.Sqrt, bias=epsc[:, 0:1], scale=1.0 / HD)
                S.op("dve", lambda e, o=ssa[:, 0:2]: e.reciprocal(o, o), [ssa[:, 0:2]], [ssa[:, 0:2]])
                for bl in range(2):
                    S.stt("dve", xn[:, bl, :], pa[:, bl, :], ssa[:, bl:bl + 1], qw[:], ALU.mult, ALU.mult)
                c_ = cs[s_]
                S.dma("sp", c_[:, :, 0, :], rowcso[i].rearrange("p (a f) -> p a f", a=2))
                rope(2, c_)
                pq = pb[1][:, 0:256].rearrange("p (a b) -> p a b", b=128)
                for bl in range(2):
                    S.tr(pq[:, bl, :], xr[:, bl, :], ident_b[:])
                S.copy("act", qkaT[s_][:, :, :], pq[:, :, :])
                S.dma("sp", QAT[:, :, i * 128:(i + 1) * 128].rearrange("h p n -> p h n"), qkaT[s_][:])
            S.phase_end()

            S.phase_begin()
            csx = S.sb([64, 7, 64], F32, "cst")
            S.dma("sp", csx[:].rearrange("p a b -> p (a b)"), cst)
            tri = [csx[:, 0, :], csx[:, 1, :]]
            msk = [csx[:, 2, :], csx[:, 3, :]]
            identf64 = csx[:, 6, :]
            identb64 = S.sb([64, 64], BF16, "identb64")
            S.copy("dve", identb64[:], identf64)
            ns8 = S.sb([64, 8, 64], F32, "ns8")
            id8 = S.sb([64, 8, 64], F32, "id8")
            for n in range(8):
                S.copy("dve", ns8[:, n, :], csx[:, 4 + n // 4, :])
                S.copy("pool", id8[:, n, :], identf64)
            gw = S.sb([128, 1], F32, "gw")
            S.dma("sp", gw[:], gnw)
            sel = S.sb([64, 4], F32, "sel")
            S.dma("sp", sel[:], seld)
            NC2 = NT * 2
            gts2 = S.sb([64, NC2, 8], F32, "gts2")
            S.dma("sp", gts2[:].rearrange("p a b -> p (a b)"), GTS)
            al = S.sb([64, 4], F32, "al")
            db = S.sb([64, 4], F32, "db")
            S.dma("sp", al[:], alog[j])
            S.dma("sp", db[:], dtb[j])
            S.act(al[:], al[:], AF.Exp)
            S.ts("dve", al[:], al[:], -1.0, None, ALU.mult)
            bt = S.sb([64, NC2, 4], F32, "bt")
            gg = S.sb([64, NC2, 4], F32, "gg")
            S.act(bt[:], gts2[:, :, 0:4], AF.Sigmoid)
            S.tt("dve", gg[:], gts2[:, :, 4:8], db[:, :].unsqueeze(1).to_broadcast([64, NC2, 4]), ALU.add)
            S.act(gg[:], gg[:], AF.Exp)
            S.act(gg[:], gg[:], AF.Ln, bias=1.0)
            S.tt("dve", gg[:], gg[:], al[:, :].unsqueeze(1).to_broadcast([64, NC2, 4]), ALU.mult)
            Sf = [[S.sb([128, 128], F32, "Sf") for h in range(2)] for d in range(2)]
            Sb = [[S.sb([128, 128], BF16, "Sb") for h in range(2)] for d in range(2)]
            for d in range(2):
                for h in range(2):
                    S.memset("pool", Sf[d][h][:], 0.0)
                    S.memset("pool", Sb[d][h][:], 0.0)
            qk = [[S.sb([128, 4, 128], BF16, "qk") for _ in range(2)] for d in range(2)]
            kv = [[S.sb([64, 8, 128], BF16, "kv") for _ in range(2)] for d in range(2)]
            gBc = S.sb([64, 8, 128], F32, "gBc")
            gcs = S.sb([64, 8], F32, "gcs")
            glb = S.sb([128, 8], F32, "glb")
            egl = S.sb([128, 8], F32, "egl")
            edl = S.sb([64, 8], F32, "edl")
            eg = S.sb([64, 8], F32, "eg")
            btn = S.sb([64, 8], F32, "btn")
            rr = S.sb([64, 8, 64], F32, "rr")
            dec = S.sb([64, 8, 64], F32, "dec")
            egB = S.sb([128, 8, 64], F32, "egB")
            A0 = S.sb([64, 8, 64], F32, "A0")
            Pk = [S.sb([64, 8, 64], F32, "Pk") for _ in range(2)]
            PTk = [S.sb([64, 8, 64], F32, "PTk") for _ in range(2)]
            TTk = [S.sb([64, 8, 64], F32, "TTk") for _ in range(2)]
            intra = S.sb([64, 8, 64], BF16, "intra")
            intraT = S.sb([64, 8, 64], BF16, "intraT")
            TTb = S.sb([64, 8, 64], BF16, "TTb")
            keg = S.sb([64, 8, 128], BF16, "keg")
            kd = S.sb([64, 8, 128], BF16, "kd")
            u_sb = S.sb([64, 8, 128], F32, "usb")
            wT = S.sb([128, 8, 64], BF16, "wT")
            qgT = S.sb([128, 8, 64], BF16, "qgT")
            vnew = [S.sb([64, 128], BF16, "vnew") for _ in range(2)]
            o_sb = [[S.sb([64, 4, 128], F32, "osb") for _ in range(2)] for d in range(2)]
            order = [list(range(NT)), [1, 0] + list(range(NT - 1, 1, -1))]
            for it in range(NT):
                s_ = it % 2
                tl = [order[0][it], order[1][it]]
                for d in range(2):
                    S.dma("sp", qk[d][s_][:].rearrange("p a b -> p (a b)"), QKT[tl[d]])
                    S.dma("act", kv[d][s_][:].rearrange("p a b -> p (a b)"), KVT[tl[d]])

                def gsl(src, d):
                    return src[:, tl[d] * 2:tl[d] * 2 + 2, d * 2:d * 2 + 2]

                for n in range(8):
                    d, c, h = n // 4, (n // 2) % 2, n % 2
                    S.ts("pool" if n % 2 else "dve", gBc[:, n, :], ones_f[0:64, :], gg[:, tl[d] * 2 + c, d * 2 + h:d * 2 + h + 1], None, ALU.mult)
                GBp = ps[0][:, :].rearrange("p (a b) -> p a b", b=64)
                for n in range(8):
                    S.mm(GBp[:, n, :], gBc[:, n, :], tri[n // 4])
                for d in range(2):
                    S.mm(ps[1][0:64, d * 4:d * 4 + 4], tri[d], gsl(gg, d))
                    S.mm(ps[1][:, 8 + d * 4:8 + d * 4 + 4], ones_f[0:64, :], gsl(gg, d))
                S.copy("dve", gcs[:], ps[1][0:64, 0:8])
                S.copy("dve", glb[:], ps[1][:, 8:16])
                S.act(egl[:], glb[:], AF.Exp)
                S.tt("dve", edl[:], glb[0:64, :], gcs[:], ALU.subtract)
                S.act(edl[:], edl[:], AF.Exp)
                S.act(eg[:], gcs[:], AF.Exp)
                for d in range(2):
                    S.copy("pool", btn[:, d * 4:d * 4 + 4].rearrange("p (a b) -> p a b", b=2), gsl(bt, d))
                for n in range(8):
                    S.stt("dve", rr[:, n, :], GBp[0:64, n, :], gcs[:, n:n + 1], msk[n // 4], ALU.subtract, ALU.max)
                S.act(dec[:], rr[:], AF.Exp, scale=-1.0)
                S.act(egB[:], GBp[:, :, :], AF.Exp)
                KKp = ps[2][0:64, :].rearrange("p (a b) -> p a b", b=64)
                QKp = ps[3][0:64, :].rearrange("p (a b) -> p a b", b=64)
                for n in range(8):
                    d, c, h = n // 4, (n // 2) % 2, n % 2
                    kTs = qk[d][s_][:, 2 + h, c * 64:(c + 1) * 64]
                    qTs = qk[d][s_][:, h, c * 64:(c + 1) * 64]
                    S.mm(KKp[:, n, :], kTs, kTs)
                    S.mm(QKp[:, n, :], qTs, kTs)
                for n in range(8):
                    S.stt("dve", A0[:, n, :], KKp[:, n, :], btn[:, n:n + 1], dec[:, n, :], ALU.mult, ALU.mult)
                S.tt("pool", Pk[0][:], A0[:], ns8[:], ALU.mult)
                S.tt("dve", intra[:], QKp[:, :, :], dec[:], ALU.mult)
                PTp = ps[4][0:64, :].rearrange("p (a b) -> p a b", b=64)
                for n in range(8):
                    S.mm(PTp[:, n, :], Pk[0][:, n, :], identf64)
                S.copy("act", PTk[0][:], PTp[:, :, :])
                S.tt("dve", TTk[0][:], PTp[:, :, :], id8[:], ALU.add)
                iTp = ps[5][0:64, :].rearrange("p (a b) -> p a b", b=64)
                for n in range(8):
                    S.mm(iTp[:, n, :], intra[:, n, :], identb64[:])
                S.copy("act", intraT[:], iTp[:, :, :])
                Pp = ps[2][0:64, :].rearrange("p (a b) -> p a b", b=64)
                PTq = ps[3][0:64, :].rearrange("p (a b) -> p a b", b=64)
                TTp = ps[4][0:64, :].rearrange("p (a b) -> p a b", b=64)
                cur = 0
                for k in range(1, 6):
                    nxt = 1 - cur
                    for n in range(8):
                        S.mm(Pp[:, n, :], PTk[cur][:, n, :], Pk[cur][:, n, :])
                    if k < 5:
                        for n in range(8):
                            S.mm(PTq[:, n, :], Pk[cur][:, n, :], PTk[cur][:, n, :])
                    S.copy("act", Pk[nxt][:], Pp[:, :, :])
                    if k < 5:
                        S.copy("dve", PTk[nxt][:], PTq[:, :, :])
                    for n in range(8):
                        S.mm(TTp[:, n, :], Pk[nxt][:, n, :], TTk[cur][:, n, :])
                    S.tt("dve", TTk[nxt][:], TTp[:, :, :], TTk[cur][:], ALU.add)
                    cur = nxt
                for n in range(8):
                    d, c, h = n // 4, (n // 2) % 2, n % 2
                    kt = kv[d][s_][:, (c * 2 + h) * 2, :]
                    S.ts("dve", TTb[:, n, :], TTk[cur][:, n, :], btn[:, n:n + 1], None, ALU.mult)
                    S.ts("pool", keg[:, n, :], kt, eg[:, n:n + 1], None, ALU.mult)
                    S.ts("pool", kd[:, n, :], kt, edl[:, n:n + 1], None, ALU.mult)
                    S.tt("pool" if n % 2 else "dve", qgT[:, n, :], qk[d][s_][:, h, c * 64:(c + 1) * 64], egB[:, n, :], ALU.mult)
                for half in range(2):
                    up = ps[5][0:64, :].rearrange("p (a b) -> p a b", b=128)
                    for m in range(4):
                        n = half * 4 + m
                        d, c, h = n // 4, (n // 2) % 2, n % 2
                        S.mm(up[:, m, :], TTb[:, n, :], kv[d][s_][:, (c * 2 + h) * 2 + 1, :])
                    S.copy("act", u_sb[:, half * 4:half * 4 + 4, :], up[:, :, :])
                wp = ps[0][:, :].rearrange("p (a b) -> p a b", b=64)
                for n in range(8):
                    S.mm(wp[:, n, :], keg[:, n, :], TTb[:, n, :])
                S.copy("act", wT[:], wp[:, :, :])
                for cc_ in range(2):
                    for d in range(2):
                        c = cc_ if d == 0 else 1 - cc_
                        for h in range(2):
                            n = d * 4 + c * 2 + h
                            vn = vnew[h]
                            pv = ps[1][0:64, 0:128]
                            po = ps[1][0:64, 128:256]
                            pS = ps[2][:, 0:128]
                            S.mm(pv, wT[:, n, :], Sb[d][h][:])
                            S.tt("dve", vn[:], u_sb[:, n, :], pv, ALU.subtract)
                            if tl[d] >= 2:
                                S.mm(po, qgT[:, n, :], Sb[d][h][:], start=True, stop=False)
                                S.mm(po, intraT[:, n, :], vn[:], start=False, stop=True)
                                S.copy("act", o_sb[d][s_][:, c * 2 + h, :], po)
                            S.mm(pS, kd[:, n, :], vn[:])
                            S.stt("dve", Sf[d][h][:], Sf[d][h][:], egl[:, n:n + 1], pS, ALU.mult, ALU.add)
                            S.copy("act", Sb[d][h][:], Sf[d][h][:])
                for d in range(2):
                    if tl[d] >= 2:
                        S.dma("sp", OD[d, tl[d] - 2], o_sb[d][s_][:].rearrange("p a b -> p (a b)"))
            S.barrier()
            ofq = [[S.sb([64, 4, 128], F32, "ofq") for q in range(4)] for _ in range(2)]
            obq = [[S.sb([64, 4, 128], F32, "obq") for q in range(4)] for _ in range(2)]
            oacc = S.sb([64, 4, 128], F32, "oacc")
            zt = [S.sb([128, 2, 128], BF16, "zt") for _ in range(2)]
            ssq = S.sb([64, 4], F32, "ssq")
            junk3 = S.sb([64, 128], F32, "junk3")
            on = S.sb([64, 4, 128], BF16, "on")
            mx = [S.sb([128, 2, 128], BF16, "mx") for _ in range(2)]
            for s in range(NO):
                s_ = s % 2
                for q in range(4):
                    S.dma("sp", ofq[s_][q][:].rearrange("p a b -> p (a b)"), OD[0, s + NO * q])
                    S.dma("act", obq[s_][q][:].rearrange("p a b -> p (a b)"), OD[1, s + NO * q])
                S.dma("sp", zt[s_][:].rearrange("p a b -> p (a b)"), ZST[s, :, 0:256])
                for q in range(4):
                    S.tt("dve", ofq[s_][q][:], ofq[s_][q][:], obq[s_][q][:], ALU.add)
                    if q == 0:
                        S.ts("dve", oacc[:], ofq[s_][q][:], sel[:, 0:1], None, ALU.mult)
                    else:
                        S.stt("dve", oacc[:], ofq[s_][q][:], sel[:, q:q + 1], oacc[:], ALU.mult, ALU.add)
                for m in range(4):
                    S.act(junk3[:], oacc[:, m, :], AF.Square, accum_out=ssq[:, m:m + 1])
                S.act(ssq[:], ssq[:], AF.Sqrt, bias=epsc[0:64, 0:1], scale=1.0 / 128)
                S.op("dve", lambda e, o=ssq[:]: e.reciprocal(o, o), [ssq[:]], [ssq[:]])
                for m in range(4):
                    S.ts("dve" if m % 2 else "pool", on[:, m, :], oacc[:, m, :], ssq[:, m:m + 1], None, ALU.mult)
                oTp = ps[0][:, 0:256].rearrange("p (a b) -> p a b", b=128)
                for c in range(2):
                    for h in range(2):
                        S.mm(oTp[:, h, c * 64:(c + 1) * 64], on[:, c * 2 + h, :], identb64[:])
                S.stt("dve", mx[s_][:], oTp[:, :, :], gw[:, 0:1], zt[s_][:], ALU.mult, ALU.mult)
                S.dma("sp", MIXT[2 * j:2 * j + 2, :, s * 128:(s + 1) * 128].rearrange("h p n -> p h n"), mx[s_][:])
            S.phase_end()

            S.phase_begin()
            QW = 512 if LO >= 512 else LO
            NQ = LO // QW
            TPQ = QW // 128
            KT = S.sb([128, NK], BF16, "KT")
            V = S.sb([128, NT, 128], BF16, "V")
            S.dma("sp", KT[:], KAT)
            for t in range(NT):
                S.dma("act" if t % 2 else "sp", V[:, t, :], VA[t])
            qt = [S.sb([128, QW], BF16, "qt") for _ in range(2)]
            zt3 = [S.sb([128, QW], BF16, "zt3") for _ in range(2)]
            pt = [S.sb([128, QW], BF16, "pt") for _ in range(2)]
            rl = S.sb([128, QW], F32, "rl")
            ot = S.sb([128, QW], F32, "ot")
            mx3 = [S.sb([128, QW], BF16, "mx3") for _ in range(2)]
            it3 = 0
            for h in range(2):
                for qi in range(NQ):
                    s_ = it3 % 2
                    it3 += 1
                    S.dma("sp", qt[s_][:], QAT[h, :, qi * QW:(qi + 1) * QW])
                    for i in range(TPQ):
                        S.dma("act", zt3[s_][:, i * 128:(i + 1) * 128], ZST[qi * TPQ + i, :, 256 + h * 128:256 + (h + 1) * 128])
                    po = ps[2 + 2 * s_][:, 0:QW]
                    pl = ps[3 + 2 * s_][:, 0:QW]
                    S.mm(ps[0][:, 0:QW], KT[:, 0:128], qt[s_][:])
                    for kt in range(NT):
                        b_ = kt % 2
                        if kt + 1 < NT:
                            S.mm(ps[1 - b_][:, 0:QW], KT[:, (kt + 1) * 128:(kt + 2) * 128], qt[s_][:])
                        S.act(pt[b_][:], ps[b_][:, 0:QW], AF.Exp)
                        S.mm(po, V[:, kt, :], pt[b_][:], start=(kt == 0), stop=(kt == NT - 1))
                        S.mm(pl, ones_b[:], pt[b_][:], start=(kt == 0), stop=(kt == NT - 1))
                    S.op("dve", lambda e, o=rl[:], i=pl: e.reciprocal(o, i), [rl[:]], [pl])
                    S.tt("dve", ot[:], po, rl[:], ALU.mult)
                    S.tt("pool", mx3[s_][:], ot[:], zt3[s_][:], ALU.mult)
                    S.dma("sp", MIXT[8 + 2 * j + h, :, qi * QW:(qi + 1) * QW], mx3[s_][:])
            S.phase_end()

        S.phase_begin()
        FW = S.sb([128, D], F32, "FW")
        S.dma("sp", FW[:], fnw)
        wo_sb = S.sb([128, NKC, D], BF16, "wosb")
        wf4 = [S.sb([128, 1024], F32, "wf4") for _ in range(2)]
        for kc in range(NKC):
            for hf in range(2):
                S.dma("sp", wf4[hf][:], wout[kc, :, hf * 1024:(hf + 1) * 1024])
                S.copy("pool" if hf else "dve", wo_sb[:, kc, hf * 1024:(hf + 1) * 1024], wf4[hf][:])
        mt = [S.sb([128, NKC, 128], BF16, "mt") for _ in range(2)]
        xt4 = [S.sb([128, D], F32, "xt4") for _ in range(2)]
        yt = S.sb([128, D], F32, "yt")
        junk4 = S.sb([128, D], BF16, "junk4")
        ss4 = S.sb([128, 1], F32, "ss4")
        ob4 = [S.sb([128, D], F32, "ob4") for _ in range(2)]
        for t in range(NO):
            s_ = t % 2
            S.dma("sp", mt[s_][:], MIXT[:, :, t * 128:(t + 1) * 128].rearrange("a p n -> p a n"))
            S.dma("act", xt4[s_][:], xo[t])
            for cb in range(4):
                for fc in range(NKC):
                    S.mm(ps[1 + cb][:, :], mt[s_][:, fc, :], wo_sb[:, fc, cb * 512:(cb + 1) * 512], start=(fc == 0), stop=(fc == NKC - 1))
                sl = slice(cb * 512, (cb + 1) * 512)
                S.tt("dve", yt[:, sl], ps[1 + cb][:, :], GT[:, sl], ALU.mult)
                S.tt("pool", yt[:, sl], yt[:, sl], xt4[s_][:, sl], ALU.add)
            S.act(junk4[:], yt[:], AF.Square, accum_out=ss4[:])
            S.act(ss4[:], ss4[:], AF.Sqrt, bias=epsc[:, 0:1], scale=1.0 / D)
            S.op("dve", lambda e, o=ss4[:]: e.reciprocal(o, o), [ss4[:]], [ss4[:]])
            S.stt("dve", ob4[s_][:], yt[:], ss4[:, 0:1], FW[:], ALU.mult, ALU.mult)
            S.dma("sp", out[t], ob4[s_][:])
        S.phase_end()
    return nc


def rope_tables(L):
    inv = (np.float32(10000.0) ** (-np.arange(32, dtype=np.float32) / np.float32(32))).astype(np.float32)
    tok = np.arange(L)
    row = (tok // 64).astype(np.float32)
    col = (tok % 64).astype(np.float32)
    ra = (row[:, None] * inv[None, :]).astype(np.float32)
    ca = (col[:, None] * inv[None, :]).astype(np.float32)
    rowcs = np.concatenate([np.cos(ra), np.sin(ra)], axis=1).astype(np.float32).reshape(L // 128, 128, 64)
    colcs = np.concatenate([np.cos(ca), np.sin(ca)], axis=1).astype(np.float32)[:128]
    return rowcs, colcs


def core_cols(j):
    hs = [2 * j, 2 * j + 1]
    kvh = j // 2
    cols = []
    for base in (0, 1024, 2048):
        for h in hs:
            cols += list(range(base + h * 128, base + (h + 1) * 128))
    for h in hs:
        cols += list(range(3072 + h * 128, 3072 + (h + 1) * 128))
    for h in hs:
        cols += list(range(5664 + h * 128, 5664 + (h + 1) * 128))
    for h in hs:
        cols += list(range(4128 + h * 128, 4128 + (h + 1) * 128))
    cols += list(range(5152 + kvh * 128, 5152 + (kvh + 1) * 128))
    cols += list(range(5408 + kvh * 128, 5408 + (kvh + 1) * 128))
    for ba in range(2):
        for d in range(2):
            for h in hs:
                cols.append(4096 + (ba * 2 + d) * 8 + h)
    return np.array(cols)


def p2_consts():
    i = np.arange(64)[:, None]
    j = np.arange(64)[None, :]
    tri0 = (i <= j).astype(np.float32)
    tri1 = (i >= j).astype(np.float32)
    m0 = np.where(i >= j, 0.0, BIG).astype(np.float32)
    m1 = np.where(i <= j, 0.0, BIG).astype(np.float32)
    ns0 = np.where(i > j, -1.0, 0.0).astype(np.float32)
    ns1 = np.where(i < j, -1.0, 0.0).astype(np.float32)
    ident = np.eye(64, dtype=np.float32)
    return np.ascontiguousarray(np.concatenate([tri0, tri1, m0, m1, ns0, ns1, ident], axis=1))


def prep_fused(inp, L):
    NO = L // 128 // 4
    rowcs, colcs = rope_tables(L)
    w_in = inp["w_in"][0]
    conv_w = inp["conv_w"][0]
    a_log = inp["a_log"][0]
    dt_bias = inp["dt_bias"][0]
    wcs, cws, als, dbs = [], [], [], []
    for j in range(4):
        cols = core_cols(j)
        wcs.append(w_in[:, cols].reshape(NKC, 128, NCOL))
        cw = conv_w[:, cols[:768]]
        cws.append(cw.T.reshape(6, 128, 5).transpose(1, 0, 2).reshape(128, 30))
        hs = [2 * j, 2 * j + 1]
        als.append(np.broadcast_to(np.array([a_log[d, h] for d in range(2) for h in hs], np.float32)[None, :], (64, 4)))
        dbs.append(np.broadcast_to(np.array([dt_bias[d, h] for d in range(2) for h in hs], np.float32)[None, :], (64, 4)))
    shared = {
        "dr_wada": np.ascontiguousarray(inp["w_ada"][0].reshape(NKC, 128, 6144)),
        "dr_bada": np.ascontiguousarray(np.broadcast_to(inp["b_ada"][0][None, :], (128, 6144))),
        "dr_normw": np.ascontiguousarray(np.broadcast_to(inp["norm_w"][0][None, :], (128, D))),
        "dr_wc": np.ascontiguousarray(np.stack(wcs)), "dr_convw": np.ascontiguousarray(np.stack(cws)),
        "dr_qnw": np.ascontiguousarray(np.broadcast_to(inp["q_norm_w"][0][None, :], (128, 128))),
        "dr_knw": np.ascontiguousarray(np.broadcast_to(inp["k_norm_w"][0][None, :], (128, 128))),
        "dr_rowcs": rowcs, "dr_colcs": colcs, "dr_ident": np.eye(128, dtype=np.float32),
        "dr_alog": np.ascontiguousarray(np.stack(als)), "dr_dtb": np.ascontiguousarray(np.stack(dbs)),
        "dr_gnw": np.ascontiguousarray(inp["gdn_norm_w"][0].reshape(128, 1)), "dr_cst": p2_consts(),
        "dr_wout": np.ascontiguousarray(inp["w_out"][0].reshape(NKC, 128, D)),
        "dr_fnw": np.ascontiguousarray(np.broadcast_to(inp["final_norm_w"][None, :], (128, D))),
    }
    maps = []
    for core in range(8):
        b, jq = core // 4, core % 4
        xh = np.concatenate([inp["ctx"][b], inp["x"][b]], axis=0).reshape(-1, 128, D)
        sel = np.zeros((64, 4), np.float32)
        sel[:, jq] = 1.0
        m = dict(shared)
        m.update({"dr_xh": np.ascontiguousarray(xh),
                  "dr_xo": np.ascontiguousarray(inp["x"][b, jq * NO * 128:(jq + 1) * NO * 128].reshape(NO, 128, D)),
                  "dr_cc": np.ascontiguousarray(np.concatenate([inp["c"][b].reshape(NKC, 128).T, inp["c_ctx"].reshape(NKC, 128).T], axis=1)),
                  "dr_rowcso": np.ascontiguousarray(rowcs[jq * NO:(jq + 1) * NO]), "dr_sel": sel})
        maps.append(m)
    return maps

from concourse.bass_utils import run_bass_kernel_spmd

_CACHE = {}


def kernel(x, c, ctx, c_ctx, w_ada, b_ada, norm_w, w_in, conv_w, a_log, dt_bias,
           gdn_norm_w, q_norm_w, k_norm_w, w_out, final_norm_w):
    inp = dict(x=np.asarray(x), c=np.asarray(c), ctx=np.asarray(ctx), c_ctx=np.asarray(c_ctx), w_ada=np.asarray(w_ada),
               b_ada=np.asarray(b_ada), norm_w=np.asarray(norm_w), w_in=np.asarray(w_in), conv_w=np.asarray(conv_w),
               a_log=np.asarray(a_log), dt_bias=np.asarray(dt_bias), gdn_norm_w=np.asarray(gdn_norm_w),
               q_norm_w=np.asarray(q_norm_w), k_norm_w=np.asarray(k_norm_w), w_out=np.asarray(w_out),
               final_norm_w=np.asarray(final_norm_w))
    B, L, _ = inp["x"].shape
    if L not in _CACHE:
        _CACHE[L] = build_fused(L)
    res = run_bass_kernel_spmd(_CACHE[L], prep_fused(inp, L), core_ids=list(range(8))).results
    NO = L // 128 // 4
    out = np.empty((B, L, 2048), np.float32)
    for core in range(8):
        b, jq = core // 4, core % 4
        out[b, jq * NO * 128:(jq + 1) * NO * 128] = np.asarray(res[core]["dr_out"]).reshape(NO * 128, 2048)
    return out
```

```python
import contextlib
import numpy as np
import concourse.bass as bass
import concourse.mybir as mybir

F32 = mybir.dt.float32
BF16 = mybir.dt.bfloat16
AF = mybir.ActivationFunctionType
ALU = mybir.AluOpType

ENGS = ("pe", "act", "dve", "pool", "sp")


class Buf:
    __slots__ = ("w", "r", "dsem")

    def __init__(self):
        self.w = None
        self.r = []
        self.dsem = None


class Sched:
    def __init__(self, nc, stack):
        self.nc = nc
        self.stack = stack
        self.prog = {e: [] for e in ENGS}
        self.nsem = 0
        self.esem = {e: self._newsem() for e in ENGS if e != "sp"}
        self.ecnt = {e: 0 for e in ENGS}
        self.waited = {e: {} for e in ENGS}
        self.dcnt = {}
        self.bufs = {}
        self.sems = {}
        self.ntens = 0
        self.psum = [nc.alloc_psum_tensor(f"psb{i}", [128, 512], F32) for i in range(6)]
        self.psum_b = [nc.alloc_psum_tensor(f"psh{i}", [128, 1024], BF16) for i in range(2)]

    def _newsem(self):
        self.nsem += 1
        s = self.stack.enter_context(self.nc.semaphore(f"s{self.nsem}"))
        return s

    def sb(self, shape, dtype, name="t"):
        self.ntens += 1
        ps_ = getattr(self, "pstack", None)
        if ps_ is not None:
            return ps_.enter_context(self.nc.sbuf_tensor(f"{name}{self.ntens}", list(shape), dtype))
        return self.nc.alloc_sbuf_tensor(f"{name}{self.ntens}", list(shape), dtype)

    def phase_begin(self):
        self.pstack = contextlib.ExitStack()
        self.phase_bufs = []
        if not hasattr(self, "free_dsems"):
            self.free_dsems = []

    def phase_end(self):
        self.finalize()
        self.prog = {e: [] for e in ENGS}
        for b in self.phase_bufs:
            if b.dsem is not None:
                self.free_dsems.append(b.dsem)
                b.dsem = None
        self.phase_bufs = []
        self.pstack.close()
        self.pstack = None

    def _buf(self, ap):
        n = ap.name
        b = self.bufs.get(n)
        if b is None:
            b = self.bufs[n] = Buf()
            if getattr(self, "pstack", None) is not None:
                self.phase_bufs.append(b)
        return b

    def _is_chip(self, ap):
        return ap.name in self.bufs or not ap.name.startswith("dr_")

    def _deps(self, reads, writes):
        ev = {}

        def add(e):
            if e is None:
                return
            s, v = e
            k = id(s)
            self.sems[k] = s
            if k in self.dcnt:
                v = 16 * self.dcnt[k]
            if ev.get(k, 0) < v:
                ev[k] = v

        for b in reads:
            add(b.w)
        for b in writes:
            add(b.w)
            for e in b.r:
                add(e)
        return ev

    def _waits(self, eng, ev):
        wd = self.waited[eng]
        for k, v in ev.items():
            if eng == "pe" and eng in self.esem and k == id(self.esem[eng]):
                continue
            if wd.get(k, 0) >= v:
                continue
            wd[k] = v
            self.prog[eng].append(("w", self.sems[k], v))

    def _done(self, e, reads, writes):
        for b in writes:
            b.w = e
            b.r = []
        for b in reads:
            b.r.append(e)

    def op(self, eng, fn, outs, ins):
        ins_ = [a for a in ins if a is not None and hasattr(a, "name") and not a.name.startswith("dr_")]
        reads = [self._buf(a) for a in ins_ if not a.name.startswith("ps")]
        writes = [self._buf(a) for a in outs] + [self._buf(a) for a in ins_ if a.name.startswith("ps")]
        self._waits(eng, self._deps(reads, writes))
        if self.ecnt[eng] >= 30000:
            self.esem[eng] = self._newsem()
            self.ecnt[eng] = 0
        self.ecnt[eng] += 1
        s = self.esem[eng]
        self.sems[id(s)] = s
        self.prog[eng].append(("i", fn, s, 1))
        self._done((s, self.ecnt[eng]), reads, writes)

    def dma(self, q, out, in_):
        o_chip = not out.name.startswith("dr_")
        i_chip = not in_.name.startswith("dr_")
        reads = [self._buf(in_)] if i_chip else []
        writes = [self._buf(out)] if o_chip else []
        tb = writes[0] if o_chip else reads[0]
        self._waits(q, self._deps(reads, writes))
        if tb.dsem is None:
            if getattr(self, "free_dsems", None):
                tb.dsem = self.free_dsems.pop()
            else:
                tb.dsem = self._newsem()
                self.dcnt[id(tb.dsem)] = 0
        s = tb.dsem
        k = id(s)
        self.sems[k] = s
        self.dcnt[k] += 1
        self.prog[q].append(("i", lambda e, o=out, i=in_: e.dma_start(out=o, in_=i), s, 16))
        self._done((s, 16 * self.dcnt[k]), reads, writes)

    def barrier(self):
        ev = {}
        for e in self.esem:
            s = self.esem[e]
            self.sems[id(s)] = s
            if self.ecnt[e]:
                ev[id(s)] = self.ecnt[e]
        for k, c in self.dcnt.items():
            if c:
                ev[k] = 16 * c
        for e in ENGS:
            self._waits(e, dict(ev))

    def mm(self, out, lhsT, rhs, start=True, stop=True):
        self.op("pe", lambda e: e.matmul(out, lhsT, rhs, start=start, stop=stop), [out], [lhsT, rhs])

    def tr(self, out, in_, ident):
        self.op("pe", lambda e: e.transpose(out, in_, ident), [out], [in_, ident])

    def act(self, out, in_, func, bias=None, scale=None, accum_out=None):
        kw = {}
        ins = [in_]
        if bias is not None:
            kw["bias"] = bias
            ins.append(bias)
        if scale is not None:
            kw["scale"] = scale
            ins.append(scale)
        outs = [out]
        if accum_out is not None:
            kw["accum_out"] = accum_out
            outs.append(accum_out)
        self.op("act", lambda e: e.activation(out, in_, func, **kw), outs, ins)

    def tt(self, eng, out, in0, in1, op):
        self.op(eng, lambda e: e.tensor_tensor(out, in0, in1, op), [out], [in0, in1])

    def ts(self, eng, out, in0, s1, s2, op0, op1=None):
        if op1 is None:
            self.op(eng, lambda e: e.tensor_scalar(out, in0, s1, None, op0), [out], [in0, s1])
        else:
            self.op(eng, lambda e: e.tensor_scalar(out, in0, s1, s2, op0, op1), [out], [in0, s1, s2])

    def stt(self, eng, out, in0, scalar, in1, op0, op1):
        self.op(eng, lambda e: e.scalar_tensor_tensor(out, in0, scalar, in1, op0, op1), [out], [in0, scalar, in1])

    def copy(self, eng, out, in_):
        if eng == "act":
            self.op("act", lambda e: e.copy(out, in_), [out], [in_])
        else:
            self.op(eng, lambda e: e.tensor_copy(out, in_), [out], [in_])

    def memset(self, eng, out, val):
        self.op(eng, lambda e: e.memset(out, val), [out], [])

    def finalize(self):
        self.barrier()
        nc = self.nc
        prog = self.prog

        def replay(name):
            def f(e):
                for it in prog[name]:
                    if it[0] == "w":
                        e.wait_ge(it[1], it[2])
                    else:
                        it[1](e).then_inc(it[2], it[3])
            return f

        with nc.Block() as block:
            block.sync(replay("sp"))
            block.scalar(replay("act"))
            block.vector(replay("dve"))
            block.gpsimd(replay("pool"))
            block.tensor(replay("pe"))


D = 2048
NKC = 16
HD = 128
EPS = 1e-6
NCOL = 1800
C_Z = 768
C_TM = 1280
C_G = 1792
BIG = 1.0e30


def build_fused(L):
    NT = 2 + L // 128
    NTL = L // 128
    NO = NTL // 4
    LO = NO * 128
    NK = NT * 128
    nc = bass.Bass("TRN2", target_bir_lowering=False)
    dt = nc.dram_tensor

    def ext(name, shape, dtype=F32):
        return dt("dr_" + name, shape, dtype, kind="ExternalInput").ap()

    xh = ext("xh", [NT, 128, D])
    xo = ext("xo", [NO, 128, D])
    cc = ext("cc", [128, 32])
    wada = ext("wada", [NKC, 128, 6144])
    bada = ext("bada", [128, 6144])
    normw = ext("normw", [128, D])
    wc = ext("wc", [4, NKC, 128, NCOL])
    convw = ext("convw", [4, 128, 30])
    qnw = ext("qnw", [128, 128])
    knw = ext("knw", [128, 128])
    rowcs = ext("rowcs", [NTL, 128, 64])
    rowcso = ext("rowcso", [NO, 128, 64])
    colcs = ext("colcs", [128, 64])
    identd = ext("ident", [128, 128])
    alog = ext("alog", [4, 64, 4])
    dtb = ext("dtb", [4, 64, 4])
    gnw = ext("gnw", [128, 1])
    cst = ext("cst", [64, 7 * 64])
    seld = ext("sel", [64, 4])
    wout = ext("wout", [NKC, 128, D])
    fnw = ext("fnw", [128, D])
    out = dt("dr_out", [NO, 128, D], F32, kind="ExternalOutput").ap()
    HTW = dt("dr_HTW", [NT, 128, NKC * 132], BF16).ap()
    HTO = dt("dr_HTO", [NO, 128, NKC * 128], BF16).ap()
    QKT = dt("dr_QKT", [NT, 128, 512], BF16).ap()
    KVT = dt("dr_KVT", [NT, 64, 1024], BF16).ap()
    ZST = dt("dr_ZST", [NO, 128, 512], BF16).ap()
    GTS = dt("dr_GTS", [64, NT * 16], F32).ap()
    QAT = dt("dr_QAT", [2, 128, LO], BF16).ap()
    KAT = dt("dr_KAT", [128, NK], BF16).ap()
    VA = dt("dr_VA", [NT, 128, 128], BF16).ap()
    OD = dt("dr_OD", [2, NTL, 64, 512], F32).ap()
    MIXT = dt("dr_MIXT", [16, 128, LO], BF16).ap()

    with contextlib.ExitStack() as stack:
        S = Sched(nc, stack)
        ps = S.psum
        pb = S.psum_b
        ident_f = S.sb([128, 128], F32, "identf")
        ident_b = S.sb([128, 128], BF16, "identb")
        ones_b = S.sb([128, 128], BF16, "onesb")
        ones_f = S.sb([128, 128], F32, "onesf")
        epsc = S.sb([128, 1], F32, "epsc")
        GT = S.sb([128, D], F32, "GT")
        S.dma("sp", ident_f[:], identd)
        S.copy("dve", ident_b[:], ident_f[:])
        S.memset("pool", ones_b[:], 1.0)
        S.memset("pool", ones_f[:], 1.0)
        S.memset("pool", epsc[:], EPS)

        S.phase_begin()
        ccs = S.sb([128, 32], F32, "ccs")
        S.dma("sp", ccs[:], cc)
        sig = S.sb([128, 32], F32, "sig")
        S.act(sig[:], ccs[:], AF.Sigmoid)
        S.tt("dve", ccs[:], ccs[:], sig[:], ALU.mult)
        G = [S.sb([128, D], F32, "G") for _ in range(2)]
        SH = [S.sb([128, D], F32, "SH") for _ in range(2)]
        nwb = S.sb([128, D], F32, "nwb")
        S.dma("sp", nwb[:], normw)
        scr = S.sb([128, 4096], F32, "scr")
        cB = scr[:, :].rearrange("p (a b) -> p a b", b=128)
        for i in range(32):
            S.ts("dve", cB[:, i, :], ones_f[:], ccs[:, i:i + 1], None, ALU.mult)
        wst = [S.sb([128, 512], F32, "wst") for _ in range(4)]
        bst = [S.sb([128, 512], F32, "bst") for _ in range(2)]
        n_ = 0
        for cb in range(12):
            nv = 2 if cb < 8 else 1
            for kc in range(NKC):
                w_ = wst[n_ % 4]
                n_ += 1
                S.dma("sp" if n_ % 2 else "act", w_[:], wada[kc, :, cb * 512:(cb + 1) * 512])
                for v in range(nv):
                    S.mm(ps[v][:, :], cB[:, v * 16 + kc, :], w_[:], start=(kc == 0), stop=(kc == NKC - 1))
            b_ = bst[cb % 2]
            S.dma("sp", b_[:], bada[:, cb * 512:(cb + 1) * 512])
            for v in range(nv):
                if cb < 4:
                    S.tt("dve", SH[v][:, cb * 512:(cb + 1) * 512], ps[v][:, :], b_[:], ALU.add)
                elif cb < 8:
                    sl = slice((cb - 4) * 512, (cb - 3) * 512)
                    S.tt("dve", G[v][:, sl], ps[v][:, :], b_[:], ALU.add)
                    S.stt("dve", G[v][:, sl], G[v][:, sl], 1.0, nwb[:, sl], ALU.add, ALU.mult)
                else:
                    sl = slice((cb - 8) * 512, (cb - 7) * 512)
                    S.tt("dve", GT[:, sl], ps[0][:, :], b_[:], ALU.add)
        xt = [S.sb([128, D], F32, "xt") for _ in range(2)]
        junk = scr[:, 2048:3072].bitcast(BF16)
        ss = [S.sb([128, 1], F32, "ss") for _ in range(2)]
        t1 = scr[:, 0:2048]
        hb = scr[:, 3072:4096].bitcast(BF16)
        W = [S.sb([128, NKC, 132], BF16, "W") for _ in range(3)]
        ho = [S.sb([128, NKC, 128], BF16, "ho") for _ in range(2)]

        def seq_of(t):
            return 0 if t < 2 else 1

        def normed(src, s_, v):
            S.dma("sp", xt[s_][:], src)
            S.act(junk, xt[s_][:], AF.Square, accum_out=ss[s_][:])
            S.act(ss[s_][:], ss[s_][:], AF.Sqrt, bias=epsc[:, 0:1], scale=1.0 / D)
            S.op("dve", lambda e, o=ss[s_][:]: e.reciprocal(o, o), [ss[s_][:]], [ss[s_][:]])
            S.stt("dve", t1, xt[s_][:], ss[s_][:], G[v][:], ALU.mult, ALU.mult)
            S.tt("pool", hb, t1, SH[v][:], ALU.add)

        for t in range(NT):
            normed(xh[t], t % 2, 0 if t >= 2 else 1)
            Wt = W[t % 3]
            for half in range(2):
                pT = pb[half][:, :].rearrange("p (a b) -> p a b", b=128)
                for k8 in range(8):
                    kc = half * 8 + k8
                    S.tr(pT[:, k8, :], hb[:, kc * 128:(kc + 1) * 128], ident_b[:])
                S.copy("act", Wt[:, half * 8:(half + 1) * 8, 2:130], pT[:, :, :])
                if t > 0 and seq_of(t - 1) == seq_of(t):
                    S.copy("act", W[(t - 1) % 3][:, half * 8:(half + 1) * 8, 130:132], pT[:, :, 0:2])
                if t + 1 < NT and seq_of(t + 1) == seq_of(t):
                    S.copy("act", W[(t + 1) % 3][:, half * 8:(half + 1) * 8, 0:2], pT[:, :, 126:128])
            if t == 0 or seq_of(t - 1) != seq_of(t):
                S.memset("pool", Wt[:, :, 0:2], 0.0)
            if t == NT - 1 or seq_of(t + 1) != seq_of(t):
                S.memset("pool", Wt[:, :, 130:132], 0.0)
            if t >= 1:
                S.dma("act", HTW[t - 1], W[(t - 1) % 3][:].rearrange("p a b -> p (a b)"))
        S.dma("act", HTW[NT - 1], W[(NT - 1) % 3][:].rearrange("p a b -> p (a b)"))
        for i in range(NO):
            normed(xo[i], i % 2, 0)
            for half in range(2):
                pT = pb[half][:, :].rearrange("p (a b) -> p a b", b=128)
                for k8 in range(8):
                    kc = half * 8 + k8
                    S.tr(pT[:, k8, :], hb[:, kc * 128:(kc + 1) * 128], ident_b[:])
                S.copy("act", ho[i % 2][:, half * 8:(half + 1) * 8, :], pT[:, :, :])
            S.dma("act", HTO[i], ho[i % 2][:].rearrange("p a b -> p (a b)"))
        S.phase_end()

        for j in range(4):
            S.phase_begin()
            cw = S.sb([128, 30], F32, "cw")
            S.dma("sp", cw[:], convw[j])
            qw = S.sb([128, 128], F32, "qw")
            kw_ = S.sb([128, 128], F32, "kw")
            S.dma("sp", qw[:], qnw)
            S.dma("sp", kw_[:], knw)
            S.ts("dve", qw[:], qw[:], float(HD ** -0.5), None, ALU.mult)
            cs = [S.sb([128, 2, 2, 32], F32, "cs") for _ in range(2)]
            for s_ in range(2):
                S.dma("sp", cs[s_][:, :, 1, :], colcs.rearrange("p (a f) -> p a f", a=2))
            w_sb = S.sb([128, NKC, NCOL], BF16, "wsb")
            wf = [S.sb([128, NCOL // 2], F32, "wf") for _ in range(2)]
            for kc in range(NKC):
                for hf in range(2):
                    S.dma("sp", wf[hf][:], wc[j, kc, :, hf * 900:(hf + 1) * 900])
                    S.copy("pool" if hf else "dve", w_sb[:, kc, hf * 900:(hf + 1) * 900], wf[hf][:])
            Wl = [S.sb([128, NKC, 132], BF16, "Wl") for _ in range(2)]
            hl = [S.sb([128, NKC, 128], BF16, "hl") for _ in range(2)]
            accs = [S.sb([128, 6, 128], F32, "acc") for _ in range(2)]
            slf = S.sb([128, 4, 128], F32, "slf")
            sq = S.sb([128, 512], BF16, "sq")
            rn = S.sb([128, 4, 128], F32, "rn")
            qkT = [S.sb([128, 4, 128], BF16, "qkT") for _ in range(2)]
            vT = S.sb([128, 2, 128], BF16, "vT")
            kv_sb = [S.sb([64, 1024], BF16, "kvsb") for _ in range(2)]
            zs = [S.sb([128, 512], BF16, "zs") for _ in range(2)]
            gts = S.sb([64, NT * 16], F32, "gts")
            ssa = S.sb([128, 4], F32, "ssa")
            junk2 = S.sb([128, 128], BF16, "junk2")
            xn = S.sb([128, 2, 128], F32, "xn")
            xr = S.sb([128, 2, 128], BF16, "xr")
            rt = [S.sb([128, 2, 32], F32, "rt") for _ in range(4)]
            va = [S.sb([128, 128], BF16, "va") for _ in range(2)]
            qkaT = [S.sb([128, 2, 128], BF16, "qkaT") for _ in range(2)]

            def rope(nblk, c_):
                Cc = c_[:, 0, :, :]
                Sn = c_[:, 1, :, :]
                for bl in range(nblk):
                    xv = xn[:, bl, :].rearrange("p (a h f) -> p a h f", a=2, h=2)
                    ov = xr[:, bl, :].rearrange("p (a h f) -> p a h f", a=2, h=2)
                    x1 = xv[:, :, 0, :]
                    x2 = xv[:, :, 1, :]
                    e1 = "dve" if bl % 2 == 0 else "pool"
                    S.tt(e1, rt[0][:], x1, Cc, ALU.mult)
                    S.tt(e1, rt[1][:], x2, Sn, ALU.mult)
                    S.tt(e1, ov[:, :, 0, :], rt[0][:], rt[1][:], ALU.subtract)
                    S.tt(e1, rt[2][:], x2, Cc, ALU.mult)
                    S.tt(e1, rt[3][:], x1, Sn, ALU.mult)
                    S.tt(e1, ov[:, :, 1, :], rt[2][:], rt[3][:], ALU.add)

            for t in range(NT):
                s_ = t % 2
                Wt = Wl[s_]
                lat = t >= 2
                S.dma("sp", Wt[:].rearrange("p a b -> p (a b)"), HTW[t])
                acc = accs[s_]
                pgs = [ps[(0 if s_ else 2) + g][:, 0:396].rearrange("p (a b) -> p a b", b=132) for g in range(2)]
                for g in range(2):
                    pg = pgs[g]
                    for bl in range(3):
                        blk = g * 3 + bl
                        for kc in range(NKC):
                            S.mm(pg[:, bl, :], w_sb[:, kc, blk * 128:(blk + 1) * 128], Wt[:, kc, :],
                                 start=(kc == 0), stop=(kc == NKC - 1))
                for g in range(2):
                    pg = pgs[g]
                    for bl in range(3):
                        blk = g * 3 + bl
                        S.ts("dve", acc[:, blk, :], pg[:, bl, 0:128], cw[:, blk * 5:blk * 5 + 1], None, ALU.mult)
                        for jj in range(1, 5):
                            S.stt("dve", acc[:, blk, :], pg[:, bl, jj:jj + 128], cw[:, blk * 5 + jj:blk * 5 + jj + 1],
                                  acc[:, blk, :], ALU.mult, ALU.add)
                S.act(slf[:, :, :], acc[:, 0:4, :], AF.Silu)
                S.act(vT[:, :, :], acc[:, 4:6, :], AF.Silu)
                S.act(sq[:], slf[:].rearrange("p a b -> p (a b)"), AF.Square)
                S.mm(ps[4][:, :], ones_b[:], sq[:])
                rnf = rn[:].rearrange("p a b -> p (a b)")
                S.act(rnf, ps[4][:, :], AF.Sqrt, bias=epsc[:, 0:1], scale=1.0)
                S.op("dve", lambda e, o=rnf: e.reciprocal(o, o), [rnf], [rnf])
                S.stt("dve", qkT[s_][:, 0:2, :], slf[:, 0:2, :], float(HD ** -0.5), rn[:, 0:2, :], ALU.mult, ALU.mult)
                S.tt("pool", qkT[s_][:, 2:4, :], slf[:, 2:4, :], rn[:, 2:4, :], ALU.mult)
                S.dma("sp", QKT[t], qkT[s_][:].rearrange("p a b -> p (a b)"))
                pkv = pb[0][0:64, :].rearrange("p (a b) -> p a b", b=128)
                for c in range(2):
                    for h in range(2):
                        S.tr(pkv[:, (c * 2 + h) * 2 + 0, :], qkT[s_][:, 2 + h, c * 64:(c + 1) * 64], ident_b[:])
                        S.tr(pkv[:, (c * 2 + h) * 2 + 1, :], vT[:, h, c * 64:(c + 1) * 64], ident_b[:])
                S.copy("act", kv_sb[s_][:], pkv.rearrange("p a b -> p (a b)"))
                S.dma("act", KVT[t], kv_sb[s_][:])
                pgt = ps[5][0:64, 0:16].rearrange("p (a b) -> p a b", b=8)
                for c in range(2):
                    for kc in range(NKC):
                        S.mm(pgt[:, c, :], Wt[:, kc, 2 + c * 64:2 + (c + 1) * 64], w_sb[:, kc, C_G:C_G + 8],
                             start=(kc == 0), stop=(kc == NKC - 1))
                S.copy("act", gts[:, t * 16:(t + 1) * 16], ps[5][0:64, 0:16])
                if j % 2 == 0:
                    pa = ps[4][:, 0:256].rearrange("p (a b) -> p a b", b=128)
                    for kc in range(NKC):
                        S.mm(ps[4][:, 0:256], Wt[:, kc, 2:130], w_sb[:, kc, C_TM + 256:C_TM + 512], start=(kc == 0), stop=(kc == NKC - 1))
                    S.copy("dve", va[s_][:], pa[:, 1, :])
                    S.dma("act", VA[t], va[s_][:])
                    S.act(junk2[:], pa[:, 0, :], AF.Square, accum_out=ssa[:, 0:1])
                    S.act(ssa[:, 0:1], ssa[:, 0:1], AF.Sqrt, bias=epsc[:, 0:1], scale=1.0 / HD)
                    S.op("dve", lambda e, o=ssa[:, 0:1]: e.reciprocal(o, o), [ssa[:, 0:1]], [ssa[:, 0:1]])
                    S.stt("dve", xn[:, 0, :], pa[:, 0, :], ssa[:, 0:1], kw_[:], ALU.mult, ALU.mult)
                    if lat:
                        c_ = cs[s_]
                        S.dma("sp", c_[:, :, 0, :], rowcs[t - 2].rearrange("p (a f) -> p a f", a=2))
                        rope(1, c_)
                    else:
                        S.copy("pool", xr[:, 0, :], xn[:, 0, :])
                    pq = pb[1][:, 0:128]
                    S.tr(pq, xr[:, 0, :], ident_b[:])
                    S.copy("act", qkaT[s_][:, 0, :], pq)
                    S.dma("sp", KAT[:, t * 128:(t + 1) * 128], qkaT[s_][:, 0, :])
            S.dma("sp", GTS, gts[:])
            for i in range(NO):
                s_ = i % 2
                S.dma("sp", hl[s_][:].rearrange("p a b -> p (a b)"), HTO[i])
                pz = ps[4][:, :].rearrange("p (a b) -> p a b", b=128)
                for bl in range(4):
                    for kc in range(NKC):
                        S.mm(pz[:, bl, :], w_sb[:, kc, C_Z + bl * 128:C_Z + (bl + 1) * 128], hl[s_][:, kc, :],
                             start=(kc == 0), stop=(kc == NKC - 1))
                S.act(zs[s_][:], ps[4][:, :], AF.Silu)
                S.dma("sp", ZST[i], zs[s_][:])
                pa = ps[5][:, 0:256].rearrange("p (a b) -> p a b", b=128)
                for kc in range(NKC):
                    S.mm(ps[5][:, 0:256], hl[s_][:, kc, :], w_sb[:, kc, C_TM:C_TM + 256], start=(kc == 0), stop=(kc == NKC - 1))
                for bl in range(2):
                    S.act(junk2[:], pa[:, bl, :], AF.Square, accum_out=ssa[:, bl:bl + 1])
                S.act(ssa[:, 0:2], ssa[:, 0:2], AF.Sqrt, bias=epsc[:, 0:1], scale=1.0 / HD)
                S.op("dve", lambda e, o=ssa[:, 0:2]: e.reciprocal(o, o), [ssa[:, 0:2]], [ssa[:, 0:2]])
                for bl in range(2):
                    S.stt("dve", xn[:, bl, :], pa[:, bl, :], ssa[:, bl:bl + 1], qw[:], ALU.mult, ALU.mult)
                c_ = cs[s_]
                S.dma("sp", c_[:, :, 0, :], rowcso[i].rearrange("p (a f) -> p a f", a=2))
                rope(2, c_)
                pq = pb[1][:, 0:256].rearrange("p (a b) -> p a b", b=128)
                for bl in range(2):
                    S.tr(pq[:, bl, :], xr[:, bl, :], ident_b[:])
                S.copy("act", qkaT[s_][:, :, :], pq[:, :, :])
                S.dma("sp", QAT[:, :, i * 128:(i + 1) * 128].rearrange("h p n -> p h n"), qkaT[s_][:])
            S.phase_end()

            S.phase_begin()
            csx = S.sb([64, 7, 64], F32, "cst")
            S.dma("sp", csx[:].rearrange("p a b -> p (a b)"), cst)
            tri = [csx[:, 0, :], csx[:, 1, :]]
            msk = [csx[:, 2, :], csx[:, 3, :]]
            identf64 = csx[:, 6, :]
            identb64 = S.sb([64, 64], BF16, "identb64")
            S.copy("dve", identb64[:], identf64)
            ns8 = S.sb([64, 8, 64], F32, "ns8")
            id8 = S.sb([64, 8, 64], F32, "id8")
            for n in range(8):
                S.copy("dve", ns8[:, n, :], csx[:, 4 + n // 4, :])
                S.copy("pool", id8[:, n, :], identf64)
            gw = S.sb([128, 1], F32, "gw")
            S.dma("sp", gw[:], gnw)
            sel = S.sb([64, 4], F32, "sel")
            S.dma("sp", sel[:], seld)
            NC2 = NT * 2
            gts2 = S.sb([64, NC2, 8], F32, "gts2")
            S.dma("sp", gts2[:].rearrange("p a b -> p (a b)"), GTS)
            al = S.sb([64, 4], F32, "al")
            db = S.sb([64, 4], F32, "db")
            S.dma("sp", al[:], alog[j])
            S.dma("sp", db[:], dtb[j])
            S.act(al[:], al[:], AF.Exp)
            S.ts("dve", al[:], al[:], -1.0, None, ALU.mult)
            bt = S.sb([64, NC2, 4], F32, "bt")
            gg = S.sb([64, NC2, 4], F32, "gg")
            S.act(bt[:], gts2[:, :, 0:4], AF.Sigmoid)
            S.tt("dve", gg[:], gts2[:, :, 4:8], db[:, :].unsqueeze(1).to_broadcast([64, NC2, 4]), ALU.add)
            S.act(gg[:], gg[:], AF.Exp)
            S.act(gg[:], gg[:], AF.Ln, bias=1.0)
            S.tt("dve", gg[:], gg[:], al[:, :].unsqueeze(1).to_broadcast([64, NC2, 4]), ALU.mult)
            Sf = [[S.sb([128, 128], F32, "Sf") for h in range(2)] for d in range(2)]
            Sb = [[S.sb([128, 128], BF16, "Sb") for h in range(2)] for d in range(2)]
            for d in range(2):
                for h in range(2):
                    S.memset("pool", Sf[d][h][:], 0.0)
                    S.memset("pool", Sb[d][h][:], 0.0)
            qk = [[S.sb([128, 4, 128], BF16, "qk") for _ in range(2)] for d in range(2)]
            kv = [[S.sb([64, 8, 128], BF16, "kv") for _ in range(2)] for d in range(2)]
            gBc = S.sb([64, 8, 128], F32, "gBc")
            gcs = S.sb([64, 8], F32, "gcs")
            glb = S.sb([128, 8], F32, "glb")
            egl = S.sb([128, 8], F32, "egl")
            edl = S.sb([64, 8], F32, "edl")
            eg = S.sb([64, 8], F32, "eg")
            btn = S.sb([64, 8], F32, "btn")
            rr = S.sb([64, 8, 64], F32, "rr")
            dec = S.sb([64, 8, 64], F32, "dec")
            egB = S.sb([128, 8, 64], F32, "egB")
            A0 = S.sb([64, 8, 64], F32, "A0")
            Pk = [[S.sb([64, 4, 64], F32, "Pk") for _ in range(2)] for g in range(2)]
            PTk = [[S.sb([64, 4, 64], F32, "PTk") for _ in range(2)] for g in range(2)]
            TTk = [[S.sb([64, 4, 64], F32, "TTk") for _ in range(2)] for g in range(2)]
            pf = [pb[0][:, :].bitcast(F32), pb[1][:, :].bitcast(F32)]
            cbank = [[ps[2], ps[3], ps[4]], [ps[5], None, None]]
            intra = S.sb([64, 8, 64], BF16, "intra")
            intraT = S.sb([64, 8, 64], BF16, "intraT")
            TTb = S.sb([64, 8, 64], BF16, "TTb")
            keg = S.sb([64, 8, 128], BF16, "keg")
            kd = S.sb([64, 8, 128], BF16, "kd")
            u_sb = S.sb([64, 8, 128], F32, "usb")
            wT = S.sb([128, 8, 64], BF16, "wT")
            qgT = S.sb([128, 8, 64], BF16, "qgT")
            vnew = [S.sb([64, 128], BF16, "vnew") for _ in range(2)]
            o_sb = [[S.sb([64, 4, 128], F32, "osb") for _ in range(2)] for d in range(2)]
            order = [list(range(NT)), [1, 0] + list(range(NT - 1, 1, -1))]
            for it in range(NT):
                s_ = it % 2
                tl = [order[0][it], order[1][it]]
                for d in range(2):
                    S.dma("sp", qk[d][s_][:].rearrange("p a b -> p (a b)"), QKT[tl[d]])
                    S.dma("act", kv[d][s_][:].rearrange("p a b -> p (a b)"), KVT[tl[d]])

                def gsl(src, d):
                    return src[:, tl[d] * 2:tl[d] * 2 + 2, d * 2:d * 2 + 2]

                for n in range(8):
                    d, c, h = n // 4, (n // 2) % 2, n % 2
                    S.ts("pool" if n % 2 else "dve", gBc[:, n, :], ones_f[0:64, :], gg[:, tl[d] * 2 + c, d * 2 + h:d * 2 + h + 1], None, ALU.mult)
                GBp = ps[0][:, :].rearrange("p (a b) -> p a b", b=64)
                for n in range(8):
                    S.mm(GBp[:, n, :], gBc[:, n, :], tri[n // 4])
                for d in range(2):
                    S.mm(ps[1][0:64, d * 4:d * 4 + 4], tri[d], gsl(gg, d))
                    S.mm(ps[1][:, 8 + d * 4:8 + d * 4 + 4], ones_f[0:64, :], gsl(gg, d))
                S.copy("dve", gcs[:], ps[1][0:64, 0:8])
                S.copy("dve", glb[:], ps[1][:, 8:16])
                S.act(egl[:], glb[:], AF.Exp)
                S.tt("dve", edl[:], glb[0:64, :], gcs[:], ALU.subtract)
                S.act(edl[:], edl[:], AF.Exp)
                S.act(eg[:], gcs[:], AF.Exp)
                for d in range(2):
                    S.copy("pool", btn[:, d * 4:d * 4 + 4].rearrange("p (a b) -> p a b", b=2), gsl(bt, d))
                for n in range(8):
                    S.stt("dve", rr[:, n, :], GBp[0:64, n, :], gcs[:, n:n + 1], msk[n // 4], ALU.subtract, ALU.max)
                S.act(dec[:], rr[:], AF.Exp, scale=-1.0)
                S.act(egB[:], GBp[:, :, :], AF.Exp)
                KKp = ps[2][0:64, :].rearrange("p (a b) -> p a b", b=64)
                QKp = ps[3][0:64, :].rearrange("p (a b) -> p a b", b=64)
                for n in range(8):
                    d, c, h = n // 4, (n // 2) % 2, n % 2
                    kTs = qk[d][s_][:, 2 + h, c * 64:(c + 1) * 64]
                    qTs = qk[d][s_][:, h, c * 64:(c + 1) * 64]
                    S.mm(KKp[:, n, :], kTs, kTs)
                    S.mm(QKp[:, n, :], qTs, kTs)
                for n in range(8):
                    S.stt("dve", A0[:, n, :], KKp[:, n, :], btn[:, n:n + 1], dec[:, n, :], ALU.mult, ALU.mult)
                S.tt("dve", intra[:], QKp[:, :, :], dec[:], ALU.mult)
                iTp = ps[5][0:64, :].rearrange("p (a b) -> p a b", b=64)
                for n in range(8):
                    S.mm(iTp[:, n, :], intra[:, n, :], identb64[:])
                S.copy("act", intraT[:], iTp[:, :, :])

                def bank(g, i):
                    if g == 0:
                        return cbank[0][i][0:64, 0:256].rearrange("p (a b) -> p a b", b=64)
                    src = ps[5] if i == 0 else pf[i - 1]
                    return src[0:64, 0:256].rearrange("p (a b) -> p a b", b=64)

                for g in range(2):
                    S.tt("pool", Pk[g][0][:], A0[:, 4 * g:4 * g + 4, :], ns8[:, 4 * g:4 * g + 4, :], ALU.mult)
                for g in range(2):
                    PTp = bank(g, 2)
                    for m in range(4):
                        S.mm(PTp[:, m, :], Pk[g][0][:, m, :], identf64)
                for g in range(2):
                    PTp = bank(g, 2)
                    S.copy("act", PTk[g][0][:], PTp[:, :, :])
                    S.tt("dve", TTk[g][0][:], PTp[:, :, :], id8[:, 0:4, :], ALU.add)
                cur = 0
                for k in range(1, 6):
                    nxt = 1 - cur
                    for g in range(2):
                        Pp, PTq = bank(g, 0), bank(g, 1)
                        for m in range(4):
                            S.mm(Pp[:, m, :], PTk[g][cur][:, m, :], Pk[g][cur][:, m, :])
                        if k < 5:
                            for m in range(4):
                                S.mm(PTq[:, m, :], Pk[g][cur][:, m, :], PTk[g][cur][:, m, :])
                    for g in range(2):
                        Pp, PTq = bank(g, 0), bank(g, 1)
                        S.copy("act", Pk[g][nxt][:], Pp[:, :, :])
                        if k < 5:
                            S.copy("dve", PTk[g][nxt][:], PTq[:, :, :])
                    for g in range(2):
                        TTp = bank(g, 2)
                        for m in range(4):
                            S.mm(TTp[:, m, :], Pk[g][nxt][:, m, :], TTk[g][cur][:, m, :])
                    for g in range(2):
                        TTp = bank(g, 2)
                        S.tt("dve", TTk[g][nxt][:], TTp[:, :, :], TTk[g][cur][:], ALU.add)
                    cur = nxt
                for n in range(8):
                    d, c, h = n // 4, (n // 2) % 2, n % 2
                    kt = kv[d][s_][:, (c * 2 + h) * 2, :]
                    S.ts("dve", TTb[:, n, :], TTk[n // 4][cur][:, n % 4, :], btn[:, n:n + 1], None, ALU.mult)
                    S.ts("pool", keg[:, n, :], kt, eg[:, n:n + 1], None, ALU.mult)
                    S.ts("pool", kd[:, n, :], kt, edl[:, n:n + 1], None, ALU.mult)
                    S.tt("pool" if n % 2 else "dve", qgT[:, n, :], qk[d][s_][:, h, c * 64:(c + 1) * 64], egB[:, n, :], ALU.mult)
                for half in range(2):
                    up = ps[5][0:64, :].rearrange("p (a b) -> p a b", b=128)
                    for m in range(4):
                        n = half * 4 + m
                        d, c, h = n // 4, (n // 2) % 2, n % 2
                        S.mm(up[:, m, :], TTb[:, n, :], kv[d][s_][:, (c * 2 + h) * 2 + 1, :])
                    S.copy("act", u_sb[:, half * 4:half * 4 + 4, :], up[:, :, :])
                wp = ps[0][:, :].rearrange("p (a b) -> p a b", b=64)
                for n in range(8):
                    S.mm(wp[:, n, :], keg[:, n, :], TTb[:, n, :])
                S.copy("act", wT[:], wp[:, :, :])
                for cc_ in range(2):
                    for d in range(2):
                        c = cc_ if d == 0 else 1 - cc_
                        for h in range(2):
                            n = d * 4 + c * 2 + h
                            vn = vnew[h]
                            pv = ps[1][0:64, 0:128]
                            po = ps[1][0:64, 128:256]
                            pS = ps[2][:, 0:128]
                            S.mm(pv, wT[:, n, :], Sb[d][h][:])
                            S.tt("dve", vn[:], u_sb[:, n, :], pv, ALU.subtract)
                            if tl[d] >= 2:
                                S.mm(po, qgT[:, n, :], Sb[d][h][:], start=True, stop=False)
                                S.mm(po, intraT[:, n, :], vn[:], start=False, stop=True)
                                S.copy("act", o_sb[d][s_][:, c * 2 + h, :], po)
                            S.mm(pS, kd[:, n, :], vn[:])
                            S.stt("dve", Sf[d][h][:], Sf[d][h][:], egl[:, n:n + 1], pS, ALU.mult, ALU.add)
                            S.copy("act", Sb[d][h][:], Sf[d][h][:])
                for d in range(2):
                    if tl[d] >= 2:
                        S.dma("sp", OD[d, tl[d] - 2], o_sb[d][s_][:].rearrange("p a b -> p (a b)"))
            S.barrier()
            ofq = [[S.sb([64, 4, 128], F32, "ofq") for q in range(4)] for _ in range(2)]
            obq = [[S.sb([64, 4, 128], F32, "obq") for q in range(4)] for _ in range(2)]
            oacc = S.sb([64, 4, 128], F32, "oacc")
            zt = [S.sb([128, 2, 128], BF16, "zt") for _ in range(2)]
            ssq = S.sb([64, 4], F32, "ssq")
            junk3 = S.sb([64, 128], F32, "junk3")
            on = S.sb([64, 4, 128], BF16, "on")
            mx = [S.sb([128, 2, 128], BF16, "mx") for _ in range(2)]
            for s in range(NO):
                s_ = s % 2
                for q in range(4):
                    S.dma("sp", ofq[s_][q][:].rearrange("p a b -> p (a b)"), OD[0, s + NO * q])
                    S.dma("act", obq[s_][q][:].rearrange("p a b -> p (a b)"), OD[1, s + NO * q])
                S.dma("sp", zt[s_][:].rearrange("p a b -> p (a b)"), ZST[s, :, 0:256])
                for q in range(4):
                    S.tt("dve", ofq[s_][q][:], ofq[s_][q][:], obq[s_][q][:], ALU.add)
                    if q == 0:
                        S.ts("dve", oacc[:], ofq[s_][q][:], sel[:, 0:1], None, ALU.mult)
                    else:
                        S.stt("dve", oacc[:], ofq[s_][q][:], sel[:, q:q + 1], oacc[:], ALU.mult, ALU.add)
                for m in range(4):
                    S.act(junk3[:], oacc[:, m, :], AF.Square, accum_out=ssq[:, m:m + 1])
                S.act(ssq[:], ssq[:], AF.Sqrt, bias=epsc[0:64, 0:1], scale=1.0 / 128)
                S.op("dve", lambda e, o=ssq[:]: e.reciprocal(o, o), [ssq[:]], [ssq[:]])
                for m in range(4):
                    S.ts("dve" if m % 2 else "pool", on[:, m, :], oacc[:, m, :], ssq[:, m:m + 1], None, ALU.mult)
                oTp = ps[0][:, 0:256].rearrange("p (a b) -> p a b", b=128)
                for c in range(2):
                    for h in range(2):
                        S.mm(oTp[:, h, c * 64:(c + 1) * 64], on[:, c * 2 + h, :], identb64[:])
                S.stt("dve", mx[s_][:], oTp[:, :, :], gw[:, 0:1], zt[s_][:], ALU.mult, ALU.mult)
                S.dma("sp", MIXT[2 * j:2 * j + 2, :, s * 128:(s + 1) * 128].rearrange("h p n -> p h n"), mx[s_][:])
            S.phase_end()

            S.phase_begin()
            QW = 512 if LO >= 512 else LO
            NQ = LO // QW
            TPQ = QW // 128
            KT = S.sb([128, NK], BF16, "KT")
            V = S.sb([128, NT, 128], BF16, "V")
            S.dma("sp", KT[:], KAT)
            for t in range(NT):
                S.dma("act" if t % 2 else "sp", V[:, t, :], VA[t])
            qt = [S.sb([128, QW], BF16, "qt") for _ in range(2)]
            zt3 = [S.sb([128, QW], BF16, "zt3") for _ in range(2)]
            pt = [S.sb([128, QW], BF16, "pt") for _ in range(2)]
            rl = S.sb([128, QW], F32, "rl")
            ot = S.sb([128, QW], F32, "ot")
            mx3 = [S.sb([128, QW], BF16, "mx3") for _ in range(2)]
            it3 = 0
            for h in range(2):
                for qi in range(NQ):
                    s_ = it3 % 2
                    it3 += 1
                    S.dma("sp", qt[s_][:], QAT[h, :, qi * QW:(qi + 1) * QW])
                    for i in range(TPQ):
                        S.dma("act", zt3[s_][:, i * 128:(i + 1) * 128], ZST[qi * TPQ + i, :, 256 + h * 128:256 + (h + 1) * 128])
                    po = ps[2 + 2 * s_][:, 0:QW]
                    pl = ps[3 + 2 * s_][:, 0:QW]
                    S.mm(ps[0][:, 0:QW], KT[:, 0:128], qt[s_][:])
                    for kt in range(NT):
                        b_ = kt % 2
                        if kt + 1 < NT:
                            S.mm(ps[1 - b_][:, 0:QW], KT[:, (kt + 1) * 128:(kt + 2) * 128], qt[s_][:])
                        S.act(pt[b_][:], ps[b_][:, 0:QW], AF.Exp)
                        S.mm(po, V[:, kt, :], pt[b_][:], start=(kt == 0), stop=(kt == NT - 1))
                        S.mm(pl, ones_b[:], pt[b_][:], start=(kt == 0), stop=(kt == NT - 1))
                    S.op("dve", lambda e, o=rl[:], i=pl: e.reciprocal(o, i), [rl[:]], [pl])
                    S.tt("dve", ot[:], po, rl[:], ALU.mult)
                    S.tt("pool", mx3[s_][:], ot[:], zt3[s_][:], ALU.mult)
                    S.dma("sp", MIXT[8 + 2 * j + h, :, qi * QW:(qi + 1) * QW], mx3[s_][:])
            S.phase_end()

        S.phase_begin()
        FW = S.sb([128, D], F32, "FW")
        S.dma("sp", FW[:], fnw)
        wo_sb = S.sb([128, NKC, D], BF16, "wosb")
        wf4 = [S.sb([128, 1024], F32, "wf4") for _ in range(2)]
        for kc in range(NKC):
            for hf in range(2):
                S.dma("sp", wf4[hf][:], wout[kc, :, hf * 1024:(hf + 1) * 1024])
                S.copy("pool" if hf else "dve", wo_sb[:, kc, hf * 1024:(hf + 1) * 1024], wf4[hf][:])
        mt = [S.sb([128, NKC, 128], BF16, "mt") for _ in range(2)]
        xt4 = [S.sb([128, D], F32, "xt4") for _ in range(2)]
        yt = S.sb([128, D], F32, "yt")
        junk4 = S.sb([128, D], BF16, "junk4")
        ss4 = S.sb([128, 1], F32, "ss4")
        ob4 = [S.sb([128, D], F32, "ob4") for _ in range(2)]
        for t in range(NO):
            s_ = t % 2
            S.dma("sp", mt[s_][:], MIXT[:, :, t * 128:(t + 1) * 128].rearrange("a p n -> p a n"))
            S.dma("act", xt4[s_][:], xo[t])
            for cb in range(4):
                for fc in range(NKC):
                    S.mm(ps[1 + cb][:, :], mt[s_][:, fc, :], wo_sb[:, fc, cb * 512:(cb + 1) * 512], start=(fc == 0), stop=(fc == NKC - 1))
                sl = slice(cb * 512, (cb + 1) * 512)
                S.tt("dve", yt[:, sl], ps[1 + cb][:, :], GT[:, sl], ALU.mult)
                S.tt("pool", yt[:, sl], yt[:, sl], xt4[s_][:, sl], ALU.add)
            S.act(junk4[:], yt[:], AF.Square, accum_out=ss4[:])
            S.act(ss4[:], ss4[:], AF.Sqrt, bias=epsc[:, 0:1], scale=1.0 / D)
            S.op("dve", lambda e, o=ss4[:]: e.reciprocal(o, o), [ss4[:]], [ss4[:]])
            S.stt("dve", ob4[s_][:], yt[:], ss4[:, 0:1], FW[:], ALU.mult, ALU.mult)
            S.dma("sp", out[t], ob4[s_][:])
        S.phase_end()
    return nc


def rope_tables(L):
    inv = (np.float32(10000.0) ** (-np.arange(32, dtype=np.float32) / np.float32(32))).astype(np.float32)
    tok = np.arange(L)
    row = (tok // 64).astype(np.float32)
    col = (tok % 64).astype(np.float32)
    ra = (row[:, None] * inv[None, :]).astype(np.float32)
    ca = (col[:, None] * inv[None, :]).astype(np.float32)
    rowcs = np.concatenate([np.cos(ra), np.sin(ra)], axis=1).astype(np.float32).reshape(L // 128, 128, 64)
    colcs = np.concatenate([np.cos(ca), np.sin(ca)], axis=1).astype(np.float32)[:128]
    return rowcs, colcs


def core_cols(j):
    hs = [2 * j, 2 * j + 1]
    kvh = j // 2
    cols = []
    for base in (0, 1024, 2048):
        for h in hs:
            cols += list(range(base + h * 128, base + (h + 1) * 128))
    for h in hs:
        cols += list(range(3072 + h * 128, 3072 + (h + 1) * 128))
    for h in hs:
        cols += list(range(5664 + h * 128, 5664 + (h + 1) * 128))
    for h in hs:
        cols += list(range(4128 + h * 128, 4128 + (h + 1) * 128))
    cols += list(range(5152 + kvh * 128, 5152 + (kvh + 1) * 128))
    cols += list(range(5408 + kvh * 128, 5408 + (kvh + 1) * 128))
    for ba in range(2):
        for d in range(2):
            for h in hs:
                cols.append(4096 + (ba * 2 + d) * 8 + h)
    return np.array(cols)


def p2_consts():
    i = np.arange(64)[:, None]
    j = np.arange(64)[None, :]
    tri0 = (i <= j).astype(np.float32)
    tri1 = (i >= j).astype(np.float32)
    m0 = np.where(i >= j, 0.0, BIG).astype(np.float32)
    m1 = np.where(i <= j, 0.0, BIG).astype(np.float32)
    ns0 = np.where(i > j, -1.0, 0.0).astype(np.float32)
    ns1 = np.where(i < j, -1.0, 0.0).astype(np.float32)
    ident = np.eye(64, dtype=np.float32)
    return np.ascontiguousarray(np.concatenate([tri0, tri1, m0, m1, ns0, ns1, ident], axis=1))


def prep_fused(inp, L):
    NO = L // 128 // 4
    rowcs, colcs = rope_tables(L)
    w_in = inp["w_in"][0]
    conv_w = inp["conv_w"][0]
    a_log = inp["a_log"][0]
    dt_bias = inp["dt_bias"][0]
    wcs, cws, als, dbs = [], [], [], []
    for j in range(4):
        cols = core_cols(j)
        wcs.append(w_in[:, cols].reshape(NKC, 128, NCOL))
        cw = conv_w[:, cols[:768]]
        cws.append(cw.T.reshape(6, 128, 5).transpose(1, 0, 2).reshape(128, 30))
        hs = [2 * j, 2 * j + 1]
        als.append(np.broadcast_to(np.array([a_log[d, h] for d in range(2) for h in hs], np.float32)[None, :], (64, 4)))
        dbs.append(np.broadcast_to(np.array([dt_bias[d, h] for d in range(2) for h in hs], np.float32)[None, :], (64, 4)))
    shared = {
        "dr_wada": np.ascontiguousarray(inp["w_ada"][0].reshape(NKC, 128, 6144)),
        "dr_bada": np.ascontiguousarray(np.broadcast_to(inp["b_ada"][0][None, :], (128, 6144))),
        "dr_normw": np.ascontiguousarray(np.broadcast_to(inp["norm_w"][0][None, :], (128, D))),
        "dr_wc": np.ascontiguousarray(np.stack(wcs)), "dr_convw": np.ascontiguousarray(np.stack(cws)),
        "dr_qnw": np.ascontiguousarray(np.broadcast_to(inp["q_norm_w"][0][None, :], (128, 128))),
        "dr_knw": np.ascontiguousarray(np.broadcast_to(inp["k_norm_w"][0][None, :], (128, 128))),
        "dr_rowcs": rowcs, "dr_colcs": colcs, "dr_ident": np.eye(128, dtype=np.float32),
        "dr_alog": np.ascontiguousarray(np.stack(als)), "dr_dtb": np.ascontiguousarray(np.stack(dbs)),
        "dr_gnw": np.ascontiguousarray(inp["gdn_norm_w"][0].reshape(128, 1)), "dr_cst": p2_consts(),
        "dr_wout": np.ascontiguousarray(inp["w_out"][0].reshape(NKC, 128, D)),
        "dr_fnw": np.ascontiguousarray(np.broadcast_to(inp["final_norm_w"][None, :], (128, D))),
    }
    maps = []
    for core in range(8):
        b, jq = core // 4, core % 4
        xh = np.concatenate([inp["ctx"][b], inp["x"][b]], axis=0).reshape(-1, 128, D)
        sel = np.zeros((64, 4), np.float32)
        sel[:, jq] = 1.0
        m = dict(shared)
        m.update({"dr_xh": np.ascontiguousarray(xh),
                  "dr_xo": np.ascontiguousarray(inp["x"][b, jq * NO * 128:(jq + 1) * NO * 128].reshape(NO, 128, D)),
                  "dr_cc": np.ascontiguousarray(np.concatenate([inp["c"][b].reshape(NKC, 128).T, inp["c_ctx"].reshape(NKC, 128).T], axis=1)),
                  "dr_rowcso": np.ascontiguousarray(rowcs[jq * NO:(jq + 1) * NO]), "dr_sel": sel})
        maps.append(m)
    return maps

from concourse.bass_utils import run_bass_kernel_spmd

_CACHE = {}


def kernel(x, c, ctx, c_ctx, w_ada, b_ada, norm_w, w_in, conv_w, a_log, dt_bias,
           gdn_norm_w, q_norm_w, k_norm_w, w_out, final_norm_w):
    inp = dict(x=np.asarray(x), c=np.asarray(c), ctx=np.asarray(ctx), c_ctx=np.asarray(c_ctx), w_ada=np.asarray(w_ada),
               b_ada=np.asarray(b_ada), norm_w=np.asarray(norm_w), w_in=np.asarray(w_in), conv_w=np.asarray(conv_w),
               a_log=np.asarray(a_log), dt_bias=np.asarray(dt_bias), gdn_norm_w=np.asarray(gdn_norm_w),
               q_norm_w=np.asarray(q_norm_w), k_norm_w=np.asarray(k_norm_w), w_out=np.asarray(w_out),
               final_norm_w=np.asarray(final_norm_w))
    B, L, _ = inp["x"].shape
    if L not in _CACHE:
        _CACHE[L] = build_fused(L)
    res = run_bass_kernel_spmd(_CACHE[L], prep_fused(inp, L), core_ids=list(range(8))).results
    NO = L // 128 // 4
    out = np.empty((B, L, 2048), np.float32)
    for core in range(8):
        b, jq = core // 4, core % 4
        out[b, jq * NO * 128:(jq + 1) * NO * 128] = np.asarray(res[core]["dr_out"]).reshape(NO * 128, 2048)
    return out
```

```python
import contextlib
import numpy as np
import concourse.bass as bass
import concourse.mybir as mybir

F32 = mybir.dt.float32
BF16 = mybir.dt.bfloat16
AF = mybir.ActivationFunctionType
ALU = mybir.AluOpType

ENGS = ("pe", "act", "dve", "pool", "sp")


class Buf:
    __slots__ = ("w", "r", "dsem")

    def __init__(self):
        self.w = None
        self.r = []
        self.dsem = None


class Sched:
    def __init__(self, nc, stack):
        self.nc = nc
        self.stack = stack
        self.prog = {e: [] for e in ENGS}
        self.nsem = 0
        self.esem = {e: self._newsem() for e in ENGS if e != "sp"}
        self.ecnt = {e: 0 for e in ENGS}
        self.waited = {e: {} for e in ENGS}
        self.dcnt = {}
        self.bufs = {}
        self.sems = {}
        self.ntens = 0
        self.psum = [nc.alloc_psum_tensor(f"psb{i}", [128, 512], F32) for i in range(6)]
        self.psum_b = [nc.alloc_psum_tensor(f"psh{i}", [128, 1024], BF16) for i in range(2)]

    def _newsem(self):
        self.nsem += 1
        s = self.stack.enter_context(self.nc.semaphore(f"s{self.nsem}"))
        return s

    def sb(self, shape, dtype, name="t"):
        self.ntens += 1
        ps_ = getattr(self, "pstack", None)
        if ps_ is not None:
            return ps_.enter_context(self.nc.sbuf_tensor(f"{name}{self.ntens}", list(shape), dtype))
        return self.nc.alloc_sbuf_tensor(f"{name}{self.ntens}", list(shape), dtype)

    def phase_begin(self):
        self.pstack = contextlib.ExitStack()
        self.phase_bufs = []
        if not hasattr(self, "free_dsems"):
            self.free_dsems = []

    def phase_end(self):
        self.finalize()
        self.prog = {e: [] for e in ENGS}
        for b in self.phase_bufs:
            if b.dsem is not None:
                self.free_dsems.append(b.dsem)
                b.dsem = None
        self.phase_bufs = []
        self.pstack.close()
        self.pstack = None

    def _buf(self, ap):
        n = ap.name
        b = self.bufs.get(n)
        if b is None:
            b = self.bufs[n] = Buf()
            if getattr(self, "pstack", None) is not None:
                self.phase_bufs.append(b)
        return b

    def _is_chip(self, ap):
        return ap.name in self.bufs or not ap.name.startswith("dr_")

    def _deps(self, reads, writes):
        ev = {}

        def add(e):
            if e is None:
                return
            s, v = e
            k = id(s)
            self.sems[k] = s
            if k in self.dcnt:
                v = 16 * self.dcnt[k]
            if ev.get(k, 0) < v:
                ev[k] = v

        for b in reads:
            add(b.w)
        for b in writes:
            add(b.w)
            for e in b.r:
                add(e)
        return ev

    def _waits(self, eng, ev):
        wd = self.waited[eng]
        for k, v in ev.items():
            if eng == "pe" and eng in self.esem and k == id(self.esem[eng]):
                continue
            if wd.get(k, 0) >= v:
                continue
            wd[k] = v
            self.prog[eng].append(("w", self.sems[k], v))

    def _done(self, e, reads, writes):
        for b in writes:
            b.w = e
            b.r = []
        for b in reads:
            b.r.append(e)

    def op(self, eng, fn, outs, ins):
        ins_ = [a for a in ins if a is not None and hasattr(a, "name") and not a.name.startswith("dr_")]
        reads = [self._buf(a) for a in ins_ if not a.name.startswith("ps")]
        writes = [self._buf(a) for a in outs] + [self._buf(a) for a in ins_ if a.name.startswith("ps")]
        self._waits(eng, self._deps(reads, writes))
        if self.ecnt[eng] >= 30000:
            self.esem[eng] = self._newsem()
            self.ecnt[eng] = 0
        self.ecnt[eng] += 1
        s = self.esem[eng]
        self.sems[id(s)] = s
        self.prog[eng].append(("i", fn, s, 1))
        self._done((s, self.ecnt[eng]), reads, writes)

    def dma(self, q, out, in_):
        o_chip = not out.name.startswith("dr_")
        i_chip = not in_.name.startswith("dr_")
        reads = [self._buf(in_)] if i_chip else []
        writes = [self._buf(out)] if o_chip else []
        tb = writes[0] if o_chip else reads[0]
        self._waits(q, self._deps(reads, writes))
        if tb.dsem is None:
            if getattr(self, "free_dsems", None):
                tb.dsem = self.free_dsems.pop()
            else:
                tb.dsem = self._newsem()
                self.dcnt[id(tb.dsem)] = 0
        s = tb.dsem
        k = id(s)
        self.sems[k] = s
        self.dcnt[k] += 1
        self.prog[q].append(("i", lambda e, o=out, i=in_: e.dma_start(out=o, in_=i), s, 16))
        self._done((s, 16 * self.dcnt[k]), reads, writes)

    def barrier(self):
        ev = {}
        for e in self.esem:
            s = self.esem[e]
            self.sems[id(s)] = s
            if self.ecnt[e]:
                ev[id(s)] = self.ecnt[e]
        for k, c in self.dcnt.items():
            if c:
                ev[k] = 16 * c
        for e in ENGS:
            self._waits(e, dict(ev))

    def mm(self, out, lhsT, rhs, start=True, stop=True):
        self.op("pe", lambda e: e.matmul(out, lhsT, rhs, start=start, stop=stop), [out], [lhsT, rhs])

    def tr(self, out, in_, ident):
        self.op("pe", lambda e: e.transpose(out, in_, ident), [out], [in_, ident])

    def act(self, out, in_, func, bias=None, scale=None, accum_out=None):
        kw = {}
        ins = [in_]
        if bias is not None:
            kw["bias"] = bias
            ins.append(bias)
        if scale is not None:
            kw["scale"] = scale
            ins.append(scale)
        outs = [out]
        if accum_out is not None:
            kw["accum_out"] = accum_out
            outs.append(accum_out)
        self.op("act", lambda e: e.activation(out, in_, func, **kw), outs, ins)

    def tt(self, eng, out, in0, in1, op):
        self.op(eng, lambda e: e.tensor_tensor(out, in0, in1, op), [out], [in0, in1])

    def ts(self, eng, out, in0, s1, s2, op0, op1=None):
        if op1 is None:
            self.op(eng, lambda e: e.tensor_scalar(out, in0, s1, None, op0), [out], [in0, s1])
        else:
            self.op(eng, lambda e: e.tensor_scalar(out, in0, s1, s2, op0, op1), [out], [in0, s1, s2])

    def stt(self, eng, out, in0, scalar, in1, op0, op1):
        self.op(eng, lambda e: e.scalar_tensor_tensor(out, in0, scalar, in1, op0, op1), [out], [in0, scalar, in1])

    def copy(self, eng, out, in_):
        if eng == "act":
            self.op("act", lambda e: e.copy(out, in_), [out], [in_])
        else:
            self.op(eng, lambda e: e.tensor_copy(out, in_), [out], [in_])

    def memset(self, eng, out, val):
        self.op(eng, lambda e: e.memset(out, val), [out], [])

    def finalize(self):
        self.barrier()
        nc = self.nc
        prog = self.prog

        def replay(name):
            def f(e):
                for it in prog[name]:
                    if it[0] == "w":
                        e.wait_ge(it[1], it[2])
                    else:
                        it[1](e).then_inc(it[2], it[3])
            return f

        with nc.Block() as block:
            block.sync(replay("sp"))
            block.scalar(replay("act"))
            block.vector(replay("dve"))
            block.gpsimd(replay("pool"))
            block.tensor(replay("pe"))


D = 2048
NKC = 16
HD = 128
EPS = 1e-6
NCOL = 1800
C_Z = 768
C_TM = 1280
C_G = 1792
BIG = 1.0e30


def build_fused(L):
    NT = 2 + L // 128
    NTL = L // 128
    NO = NTL // 4
    LO = NO * 128
    NK = NT * 128
    nc = bass.Bass("TRN2", target_bir_lowering=False)
    dt = nc.dram_tensor

    def ext(name, shape, dtype=F32):
        return dt("dr_" + name, shape, dtype, kind="ExternalInput").ap()

    xh = ext("xh", [NT, 128, D])
    xo = ext("xo", [NO, 128, D])
    cc = ext("cc", [128, 32])
    wada = ext("wada", [NKC, 128, 6144])
    bada = ext("bada", [128, 6144])
    normw = ext("normw", [128, D])
    wc = ext("wc", [4, NKC, 128, NCOL])
    convw = ext("convw", [4, 128, 30])
    qnw = ext("qnw", [128, 128])
    knw = ext("knw", [128, 128])
    rowcs = ext("rowcs", [NTL, 128, 64])
    rowcso = ext("rowcso", [NO, 128, 64])
    colcs = ext("colcs", [128, 64])
    identd = ext("ident", [128, 128])
    alog = ext("alog", [4, 64, 4])
    dtb = ext("dtb", [4, 64, 4])
    gnw = ext("gnw", [128, 1])
    cst = ext("cst", [64, 7 * 64])
    seld = ext("sel", [64, 4])
    wout = ext("wout", [NKC, 128, D])
    fnw = ext("fnw", [128, D])
    out = dt("dr_out", [NO, 128, D], F32, kind="ExternalOutput").ap()
    HTW = dt("dr_HTW", [NT, 128, NKC * 132], BF16).ap()
    HTO = dt("dr_HTO", [NO, 128, NKC * 128], BF16).ap()
    QKT = dt("dr_QKT", [NT, 128, 512], BF16).ap()
    KVT = dt("dr_KVT", [NT, 64, 1024], BF16).ap()
    ZST = dt("dr_ZST", [NO, 128, 512], BF16).ap()
    GTS = dt("dr_GTS", [64, NT * 16], F32).ap()
    QAT = dt("dr_QAT", [2, 128, LO], BF16).ap()
    KAT = dt("dr_KAT", [128, NK], BF16).ap()
    VA = dt("dr_VA", [NT, 128, 128], BF16).ap()
    OD = dt("dr_OD", [2, NTL, 64, 512], F32).ap()
    MIXT = dt("dr_MIXT", [16, 128, LO], BF16).ap()

    with contextlib.ExitStack() as stack:
        S = Sched(nc, stack)
        ps = S.psum
        pb = S.psum_b
        ident_f = S.sb([128, 128], F32, "identf")
        ident_b = S.sb([128, 128], BF16, "identb")
        ones_b = S.sb([128, 128], BF16, "onesb")
        ones_f = S.sb([128, 128], F32, "onesf")
        epsc = S.sb([128, 1], F32, "epsc")
        GT = S.sb([128, D], F32, "GT")
        S.dma("sp", ident_f[:], identd)
        S.copy("dve", ident_b[:], ident_f[:])
        S.memset("pool", ones_b[:], 1.0)
        S.memset("pool", ones_f[:], 1.0)
        S.memset("pool", epsc[:], EPS)

        S.phase_begin()
        ccs = S.sb([128, 32], F32, "ccs")
        S.dma("sp", ccs[:], cc)
        sig = S.sb([128, 32], F32, "sig")
        S.act(sig[:], ccs[:], AF.Sigmoid)
        S.tt("dve", ccs[:], ccs[:], sig[:], ALU.mult)
        G = [S.sb([128, D], F32, "G") for _ in range(2)]
        SH = [S.sb([128, D], F32, "SH") for _ in range(2)]
        nwb = S.sb([128, D], F32, "nwb")
        S.dma("sp", nwb[:], normw)
        scr = S.sb([128, 4096], F32, "scr")
        cB = scr[:, :].rearrange("p (a b) -> p a b", b=128)
        for i in range(32):
            S.ts("dve", cB[:, i, :], ones_f[:], ccs[:, i:i + 1], None, ALU.mult)
        wst = [S.sb([128, 512], F32, "wst") for _ in range(4)]
        bst = [S.sb([128, 512], F32, "bst") for _ in range(2)]
        n_ = 0
        for cb in range(12):
            nv = 2 if cb < 8 else 1
            for kc in range(NKC):
                w_ = wst[n_ % 4]
                n_ += 1
                S.dma("sp" if n_ % 2 else "act", w_[:], wada[kc, :, cb * 512:(cb + 1) * 512])
                for v in range(nv):
                    S.mm(ps[v][:, :], cB[:, v * 16 + kc, :], w_[:], start=(kc == 0), stop=(kc == NKC - 1))
            b_ = bst[cb % 2]
            S.dma("sp", b_[:], bada[:, cb * 512:(cb + 1) * 512])
            for v in range(nv):
                if cb < 4:
                    S.tt("dve", SH[v][:, cb * 512:(cb + 1) * 512], ps[v][:, :], b_[:], ALU.add)
                elif cb < 8:
                    sl = slice((cb - 4) * 512, (cb - 3) * 512)
                    S.tt("dve", G[v][:, sl], ps[v][:, :], b_[:], ALU.add)
                    S.stt("dve", G[v][:, sl], G[v][:, sl], 1.0, nwb[:, sl], ALU.add, ALU.mult)
                else:
                    sl = slice((cb - 8) * 512, (cb - 7) * 512)
                    S.tt("dve", GT[:, sl], ps[0][:, :], b_[:], ALU.add)
        xt = [S.sb([128, D], F32, "xt") for _ in range(2)]
        junk = scr[:, 2048:3072].bitcast(BF16)
        ss = [S.sb([128, 1], F32, "ss") for _ in range(2)]
        t1 = scr[:, 0:2048]
        hb = scr[:, 3072:4096].bitcast(BF16)
        W = [S.sb([128, NKC, 132], BF16, "W") for _ in range(3)]
        ho = [S.sb([128, NKC, 128], BF16, "ho") for _ in range(2)]

        def seq_of(t):
            return 0 if t < 2 else 1

        def normed(src, s_, v):
            S.dma("sp", xt[s_][:], src)
            S.act(junk, xt[s_][:], AF.Square, accum_out=ss[s_][:])
            S.act(ss[s_][:], ss[s_][:], AF.Sqrt, bias=epsc[:, 0:1], scale=1.0 / D)
            S.op("dve", lambda e, o=ss[s_][:]: e.reciprocal(o, o), [ss[s_][:]], [ss[s_][:]])
            S.stt("dve", t1, xt[s_][:], ss[s_][:], G[v][:], ALU.mult, ALU.mult)
            S.tt("pool", hb, t1, SH[v][:], ALU.add)

        for t in range(NT):
            normed(xh[t], t % 2, 0 if t >= 2 else 1)
            Wt = W[t % 3]
            for half in range(2):
                pT = pb[half][:, :].rearrange("p (a b) -> p a b", b=128)
                for k8 in range(8):
                    kc = half * 8 + k8
                    S.tr(pT[:, k8, :], hb[:, kc * 128:(kc + 1) * 128], ident_b[:])
                S.copy("act", Wt[:, half * 8:(half + 1) * 8, 2:130], pT[:, :, :])
                if t > 0 and seq_of(t - 1) == seq_of(t):
                    S.copy("act", W[(t - 1) % 3][:, half * 8:(half + 1) * 8, 130:132], pT[:, :, 0:2])
                if t + 1 < NT and seq_of(t + 1) == seq_of(t):
                    S.copy("act", W[(t + 1) % 3][:, half * 8:(half + 1) * 8, 0:2], pT[:, :, 126:128])
            if t == 0 or seq_of(t - 1) != seq_of(t):
                S.memset("pool", Wt[:, :, 0:2], 0.0)
            if t == NT - 1 or seq_of(t + 1) != seq_of(t):
                S.memset("pool", Wt[:, :, 130:132], 0.0)
            if t >= 1:
                S.dma("act", HTW[t - 1], W[(t - 1) % 3][:].rearrange("p a b -> p (a b)"))
        S.dma("act", HTW[NT - 1], W[(NT - 1) % 3][:].rearrange("p a b -> p (a b)"))
        for i in range(NO):
            normed(xo[i], i % 2, 0)
            for half in range(2):
                pT = pb[half][:, :].rearrange("p (a b) -> p a b", b=128)
                for k8 in range(8):
                    kc = half * 8 + k8
                    S.tr(pT[:, k8, :], hb[:, kc * 128:(kc + 1) * 128], ident_b[:])
                S.copy("act", ho[i % 2][:, half * 8:(half + 1) * 8, :], pT[:, :, :])
            S.dma("act", HTO[i], ho[i % 2][:].rearrange("p a b -> p (a b)"))
        S.phase_end()

        for j in range(4):
            S.phase_begin()
            cw = S.sb([128, 30], F32, "cw")
            S.dma("sp", cw[:], convw[j])
            qw = S.sb([128, 128], F32, "qw")
            kw_ = S.sb([128, 128], F32, "kw")
            S.dma("sp", qw[:], qnw)
            S.dma("sp", kw_[:], knw)
            S.ts("dve", qw[:], qw[:], float(HD ** -0.5), None, ALU.mult)
            cs = [S.sb([128, 2, 2, 32], F32, "cs") for _ in range(2)]
            for s_ in range(2):
                S.dma("sp", cs[s_][:, :, 1, :], colcs.rearrange("p (a f) -> p a f", a=2))
            w_sb = S.sb([128, NKC, NCOL], BF16, "wsb")
            wf = [S.sb([128, NCOL // 2], F32, "wf") for _ in range(2)]
            for kc in range(NKC):
                for hf in range(2):
                    S.dma("sp", wf[hf][:], wc[j, kc, :, hf * 900:(hf + 1) * 900])
                    S.copy("pool" if hf else "dve", w_sb[:, kc, hf * 900:(hf + 1) * 900], wf[hf][:])
            Wl = [S.sb([128, NKC, 132], BF16, "Wl") for _ in range(2)]
            hl = [S.sb([128, NKC, 128], BF16, "hl") for _ in range(2)]
            accs = [S.sb([128, 6, 128], F32, "acc") for _ in range(2)]
            slf = S.sb([128, 4, 128], F32, "slf")
            sq = S.sb([128, 512], BF16, "sq")
            rn = S.sb([128, 4, 128], F32, "rn")
            qkT = [S.sb([128, 4, 128], BF16, "qkT") for _ in range(2)]
            vT = S.sb([128, 2, 128], BF16, "vT")
            kv_sb = [S.sb([64, 1024], BF16, "kvsb") for _ in range(2)]
            zs = [S.sb([128, 512], BF16, "zs") for _ in range(2)]
            gts = S.sb([64, NT * 16], F32, "gts")
            ssa = S.sb([128, 4], F32, "ssa")
            junk2 = S.sb([128, 128], BF16, "junk2")
            xn = S.sb([128, 2, 128], F32, "xn")
            xr = S.sb([128, 2, 128], BF16, "xr")
            rt = [S.sb([128, 2, 32], F32, "rt") for _ in range(4)]
            va = [S.sb([128, 128], BF16, "va") for _ in range(2)]
            qkaT = [S.sb([128, 2, 128], BF16, "qkaT") for _ in range(2)]

            def rope(nblk, c_):
                Cc = c_[:, 0, :, :]
                Sn = c_[:, 1, :, :]
                for bl in range(nblk):
                    xv = xn[:, bl, :].rearrange("p (a h f) -> p a h f", a=2, h=2)
                    ov = xr[:, bl, :].rearrange("p (a h f) -> p a h f", a=2, h=2)
                    x1 = xv[:, :, 0, :]
                    x2 = xv[:, :, 1, :]
                    e1 = "dve" if bl % 2 == 0 else "pool"
                    S.tt(e1, rt[0][:], x1, Cc, ALU.mult)
                    S.tt(e1, rt[1][:], x2, Sn, ALU.mult)
                    S.tt(e1, ov[:, :, 0, :], rt[0][:], rt[1][:], ALU.subtract)
                    S.tt(e1, rt[2][:], x2, Cc, ALU.mult)
                    S.tt(e1, rt[3][:], x1, Sn, ALU.mult)
                    S.tt(e1, ov[:, :, 1, :], rt[2][:], rt[3][:], ALU.add)

            for t in range(NT):
                s_ = t % 2
                Wt = Wl[s_]
                lat = t >= 2
                S.dma("sp", Wt[:].rearrange("p a b -> p (a b)"), HTW[t])
                acc = accs[s_]
                pgs = [ps[(0 if s_ else 2) + g][:, 0:396].rearrange("p (a b) -> p a b", b=132) for g in range(2)]
                for g in range(2):
                    pg = pgs[g]
                    for bl in range(3):
                        blk = g * 3 + bl
                        for kc in range(NKC):
                            S.mm(pg[:, bl, :], w_sb[:, kc, blk * 128:(blk + 1) * 128], Wt[:, kc, :],
                                 start=(kc == 0), stop=(kc == NKC - 1))
                for g in range(2):
                    pg = pgs[g]
                    for bl in range(3):
                        blk = g * 3 + bl
                        S.ts("dve", acc[:, blk, :], pg[:, bl, 0:128], cw[:, blk * 5:blk * 5 + 1], None, ALU.mult)
                        for jj in range(1, 5):
                            S.stt("dve", acc[:, blk, :], pg[:, bl, jj:jj + 128], cw[:, blk * 5 + jj:blk * 5 + jj + 1],
                                  acc[:, blk, :], ALU.mult, ALU.add)
                S.act(slf[:, :, :], acc[:, 0:4, :], AF.Silu)
                S.act(vT[:, :, :], acc[:, 4:6, :], AF.Silu)
                S.act(sq[:], slf[:].rearrange("p a b -> p (a b)"), AF.Square)
                S.mm(ps[4][:, :], ones_b[:], sq[:])
                rnf = rn[:].rearrange("p a b -> p (a b)")
                S.act(rnf, ps[4][:, :], AF.Sqrt, bias=epsc[:, 0:1], scale=1.0)
                S.op("dve", lambda e, o=rnf: e.reciprocal(o, o), [rnf], [rnf])
                S.stt("dve", qkT[s_][:, 0:2, :], slf[:, 0:2, :], float(HD ** -0.5), rn[:, 0:2, :], ALU.mult, ALU.mult)
                S.tt("pool", qkT[s_][:, 2:4, :], slf[:, 2:4, :], rn[:, 2:4, :], ALU.mult)
                S.dma("sp", QKT[t], qkT[s_][:].rearrange("p a b -> p (a b)"))
                pkv = pb[0][0:64, :].rearrange("p (a b) -> p a b", b=128)
                for c in range(2):
                    for h in range(2):
                        S.tr(pkv[:, (c * 2 + h) * 2 + 0, :], qkT[s_][:, 2 + h, c * 64:(c + 1) * 64], ident_b[:])
                        S.tr(pkv[:, (c * 2 + h) * 2 + 1, :], vT[:, h, c * 64:(c + 1) * 64], ident_b[:])
                S.copy("act", kv_sb[s_][:], pkv.rearrange("p a b -> p (a b)"))
                S.dma("act", KVT[t], kv_sb[s_][:])
                pgt = ps[5][0:64, 0:16].rearrange("p (a b) -> p a b", b=8)
                for c in range(2):
                    for kc in range(NKC):
                        S.mm(pgt[:, c, :], Wt[:, kc, 2 + c * 64:2 + (c + 1) * 64], w_sb[:, kc, C_G:C_G + 8],
                             start=(kc == 0), stop=(kc == NKC - 1))
                S.copy("act", gts[:, t * 16:(t + 1) * 16], ps[5][0:64, 0:16])
                if j % 2 == 0:
                    pa = ps[4][:, 0:256].rearrange("p (a b) -> p a b", b=128)
                    for kc in range(NKC):
                        S.mm(ps[4][:, 0:256], Wt[:, kc, 2:130], w_sb[:, kc, C_TM + 256:C_TM + 512], start=(kc == 0), stop=(kc == NKC - 1))
                    S.copy("dve", va[s_][:], pa[:, 1, :])
                    S.dma("act", VA[t], va[s_][:])
                    S.act(junk2[:], pa[:, 0, :], AF.Square, accum_out=ssa[:, 0:1])
                    S.act(ssa[:, 0:1], ssa[:, 0:1], AF.Sqrt, bias=epsc[:, 0:1], scale=1.0 / HD)
                    S.op("dve", lambda e, o=ssa[:, 0:1]: e.reciprocal(o, o), [ssa[:, 0:1]], [ssa[:, 0:1]])
                    S.stt("dve", xn[:, 0, :], pa[:, 0, :], ssa[:, 0:1], kw_[:], ALU.mult, ALU.mult)
                    if lat:
                        c_ = cs[s_]
                        S.dma("sp", c_[:, :, 0, :], rowcs[t - 2].rearrange("p (a f) -> p a f", a=2))
                        rope(1, c_)
                    else:
                        S.copy("pool", xr[:, 0, :], xn[:, 0, :])
                    pq = pb[1][:, 0:128]
                    S.tr(pq, xr[:, 0, :], ident_b[:])
                    S.copy("act", qkaT[s_][:, 0, :], pq)
                    S.dma("sp", KAT[:, t * 128:(t + 1) * 128], qkaT[s_][:, 0, :])
            S.dma("sp", GTS, gts[:])
            for i in range(NO):
                s_ = i % 2
                S.dma("sp", hl[s_][:].rearrange("p a b -> p (a b)"), HTO[i])
                pz = ps[4][:, :].rearrange("p (a b) -> p a b", b=128)
                for bl in range(4):
                    for kc in range(NKC):
                        S.mm(pz[:, bl, :], w_sb[:, kc, C_Z + bl * 128:C_Z + (bl + 1) * 128], hl[s_][:, kc, :],
                             start=(kc == 0), stop=(kc == NKC - 1))
                S.act(zs[s_][:], ps[4][:, :], AF.Silu)
                S.dma("sp", ZST[i], zs[s_][:])
                pa = ps[5][:, 0:256].rearrange("p (a b) -> p a b", b=128)
                for kc in range(NKC):
                    S.mm(ps[5][:, 0:256], hl[s_][:, kc, :], w_sb[:, kc, C_TM:C_TM + 256], start=(kc == 0), stop=(kc == NKC - 1))
                for bl in range(2):
                    S.act(junk2[:], pa[:, bl, :], AF.Square, accum_out=ssa[:, bl:bl + 1])
                S.act(ssa[:, 0:2], ssa[:, 0:2], AF.Sqrt, bias=epsc[:, 0:1], scale=1.0 / HD)
                S.op("dve", lambda e, o=ssa[:, 0:2]: e.reciprocal(o, o), [ssa[:, 0:2]], [ssa[:, 0:2]])
                for bl in range(2):
                    S.stt("dve", xn[:, bl, :], pa[:, bl, :], ssa[:, bl:bl + 1], qw[:], ALU.mult, ALU.mult)
                c_ = cs[s_]
                S.dma("sp", c_[:, :, 0, :], rowcso[i].rearrange("p (a f) -> p a f", a=2))
                rope(2, c_)
                pq = pb[1][:, 0:256].rearrange("p (a b) -> p a b", b=128)
                for bl in range(2):
                    S.tr(pq[:, bl, :], xr[:, bl, :], ident_b[:])
                S.copy("act", qkaT[s_][:, :, :], pq[:, :, :])
                S.dma("sp", QAT[:, :, i * 128:(i + 1) * 128].rearrange("h p n -> p h n"), qkaT[s_][:])
            S.phase_end()

            S.phase_begin()
            csx = S.sb([64, 7, 64], F32, "cst")
            S.dma("sp", csx[:].rearrange("p a b -> p (a b)"), cst)
            tri = [csx[:, 0, :], csx[:, 1, :]]
            msk = [csx[:, 2, :], csx[:, 3, :]]
            identf64 = csx[:, 6, :]
            identb64 = S.sb([64, 64], BF16, "identb64")
            S.copy("dve", identb64[:], identf64)
            ns8 = S.sb([64, 8, 64], F32, "ns8")
            id8 = S.sb([64, 8, 64], F32, "id8")
            for n in range(8):
                S.copy("dve", ns8[:, n, :], csx[:, 4 + n // 4, :])
                S.copy("pool", id8[:, n, :], identf64)
            gw = S.sb([128, 1], F32, "gw")
            S.dma("sp", gw[:], gnw)
            sel = S.sb([64, 4], F32, "sel")
            S.dma("sp", sel[:], seld)
            NC2 = NT * 2
            gts2 = S.sb([64, NC2, 8], F32, "gts2")
            S.dma("sp", gts2[:].rearrange("p a b -> p (a b)"), GTS)
            al = S.sb([64, 4], F32, "al")
            db = S.sb([64, 4], F32, "db")
            S.dma("sp", al[:], alog[j])
            S.dma("sp", db[:], dtb[j])
            S.act(al[:], al[:], AF.Exp)
            S.ts("dve", al[:], al[:], -1.0, None, ALU.mult)
            bt = S.sb([64, NC2, 4], F32, "bt")
            gg = S.sb([64, NC2, 4], F32, "gg")
            S.act(bt[:], gts2[:, :, 0:4], AF.Sigmoid)
            S.tt("dve", gg[:], gts2[:, :, 4:8], db[:, :].unsqueeze(1).to_broadcast([64, NC2, 4]), ALU.add)
            S.act(gg[:], gg[:], AF.Exp)
            S.act(gg[:], gg[:], AF.Ln, bias=1.0)
            S.tt("dve", gg[:], gg[:], al[:, :].unsqueeze(1).to_broadcast([64, NC2, 4]), ALU.mult)
            Sf = [[S.sb([128, 128], F32, "Sf") for h in range(2)] for d in range(2)]
            Sb = [[S.sb([128, 128], BF16, "Sb") for h in range(2)] for d in range(2)]
            for d in range(2):
                for h in range(2):
                    S.memset("pool", Sf[d][h][:], 0.0)
                    S.memset("pool", Sb[d][h][:], 0.0)
            qk = [[S.sb([128, 4, 128], BF16, "qk") for _ in range(2)] for d in range(2)]
            kv = [[S.sb([64, 8, 128], BF16, "kv") for _ in range(2)] for d in range(2)]
            gBc = S.sb([64, 8, 128], F32, "gBc")
            gcs = S.sb([64, 8], F32, "gcs")
            glb = S.sb([128, 8], F32, "glb")
            egl = S.sb([128, 8], F32, "egl")
            edl = S.sb([64, 8], F32, "edl")
            eg = S.sb([64, 8], F32, "eg")
            btn = S.sb([64, 8], F32, "btn")
            rr = S.sb([64, 8, 64], F32, "rr")
            dec = S.sb([64, 8, 64], F32, "dec")
            egB = S.sb([128, 8, 64], F32, "egB")
            A0 = S.sb([64, 8, 64], F32, "A0")
            Pk = [[S.sb([64, 4, 64], F32, "Pk") for _ in range(2)] for g in range(2)]
            PTk = [[S.sb([64, 4, 64], F32, "PTk") for _ in range(2)] for g in range(2)]
            TTk = [[S.sb([64, 4, 64], F32, "TTk") for _ in range(2)] for g in range(2)]
            pf = [pb[0][:, :].bitcast(F32), pb[1][:, :].bitcast(F32)]
            cbank = [[ps[2], ps[3], ps[4]], [ps[5], None, None]]
            intra = S.sb([64, 8, 64], BF16, "intra")
            intraT = S.sb([64, 8, 64], BF16, "intraT")
            TTb = S.sb([64, 8, 64], BF16, "TTb")
            keg = S.sb([64, 8, 128], BF16, "keg")
            kd = S.sb([64, 8, 128], BF16, "kd")
            u_sb = S.sb([64, 8, 128], F32, "usb")
            wT = S.sb([128, 8, 64], BF16, "wT")
            qgT = S.sb([128, 8, 64], BF16, "qgT")
            vnew = [S.sb([64, 128], BF16, "vnew") for _ in range(2)]
            o_sb = [[S.sb([64, 4, 128], F32, "osb") for _ in range(2)] for d in range(2)]
            order = [list(range(NT)), [1, 0] + list(range(NT - 1, 1, -1))]
            for it in range(NT):
                s_ = it % 2
                tl = [order[0][it], order[1][it]]
                for d in range(2):
                    S.dma("sp", qk[d][s_][:].rearrange("p a b -> p (a b)"), QKT[tl[d]])
                    S.dma("act", kv[d][s_][:].rearrange("p a b -> p (a b)"), KVT[tl[d]])

                def gsl(src, d):
                    return src[:, tl[d] * 2:tl[d] * 2 + 2, d * 2:d * 2 + 2]

                for n in range(8):
                    d, c, h = n // 4, (n // 2) % 2, n % 2
                    S.ts("pool" if n % 2 else "dve", gBc[:, n, :], ones_f[0:64, :], gg[:, tl[d] * 2 + c, d * 2 + h:d * 2 + h + 1], None, ALU.mult)
                GBp = ps[0][:, :].rearrange("p (a b) -> p a b", b=64)
                for n in range(8):
                    S.mm(GBp[:, n, :], gBc[:, n, :], tri[n // 4])
                for d in range(2):
                    S.mm(ps[1][0:64, d * 4:d * 4 + 4], tri[d], gsl(gg, d))
                    S.mm(ps[1][:, 8 + d * 4:8 + d * 4 + 4], ones_f[0:64, :], gsl(gg, d))
                S.copy("dve", gcs[:], ps[1][0:64, 0:8])
                S.copy("dve", glb[:], ps[1][:, 8:16])
                S.act(egl[:], glb[:], AF.Exp)
                S.tt("dve", edl[:], glb[0:64, :], gcs[:], ALU.subtract)
                S.act(edl[:], edl[:], AF.Exp)
                S.act(eg[:], gcs[:], AF.Exp)
                for d in range(2):
                    S.copy("pool", btn[:, d * 4:d * 4 + 4].rearrange("p (a b) -> p a b", b=2), gsl(bt, d))
                for n in range(8):
                    S.stt("dve", rr[:, n, :], GBp[0:64, n, :], gcs[:, n:n + 1], msk[n // 4], ALU.subtract, ALU.max)
                S.act(dec[:], rr[:], AF.Exp, scale=-1.0)
                S.act(egB[:], GBp[:, :, :], AF.Exp)
                KKp = ps[2][0:64, :].rearrange("p (a b) -> p a b", b=64)
                QKp = ps[3][0:64, :].rearrange("p (a b) -> p a b", b=64)
                for n in range(8):
                    d, c, h = n // 4, (n // 2) % 2, n % 2
                    kTs = qk[d][s_][:, 2 + h, c * 64:(c + 1) * 64]
                    qTs = qk[d][s_][:, h, c * 64:(c + 1) * 64]
                    S.mm(KKp[:, n, :], kTs, kTs)
                    S.mm(QKp[:, n, :], qTs, kTs)
                for n in range(8):
                    S.stt("dve", A0[:, n, :], KKp[:, n, :], btn[:, n:n + 1], dec[:, n, :], ALU.mult, ALU.mult)
                S.tt("dve", intra[:], QKp[:, :, :], dec[:], ALU.mult)
                iTp = ps[5][0:64, :].rearrange("p (a b) -> p a b", b=64)
                for n in range(8):
                    S.mm(iTp[:, n, :], intra[:, n, :], identb64[:])
                S.copy("act", intraT[:], iTp[:, :, :])

                def bank(g, i):
                    if g == 0:
                        return cbank[0][i][0:64, 0:256].rearrange("p (a b) -> p a b", b=64)
                    src = ps[5] if i == 0 else pf[i - 1]
                    return src[0:64, 0:256].rearrange("p (a b) -> p a b", b=64)

                for g in range(2):
                    S.tt("pool", Pk[g][0][:], A0[:, 4 * g:4 * g + 4, :], ns8[:, 4 * g:4 * g + 4, :], ALU.mult)
                for g in range(2):
                    PTp = bank(g, 2)
                    for m in range(4):
                        S.mm(PTp[:, m, :], Pk[g][0][:, m, :], identf64)
                for g in range(2):
                    PTp = bank(g, 2)
                    S.copy("act", PTk[g][0][:], PTp[:, :, :])
                    S.tt("dve", TTk[g][0][:], PTp[:, :, :], id8[:, 0:4, :], ALU.add)
                cur = 0
                for k in range(1, 6):
                    nxt = 1 - cur
                    for g in range(2):
                        Pp, PTq = bank(g, 0), bank(g, 1)
                        for m in range(4):
                            S.mm(Pp[:, m, :], PTk[g][cur][:, m, :], Pk[g][cur][:, m, :])
                        if k < 5:
                            for m in range(4):
                                S.mm(PTq[:, m, :], Pk[g][cur][:, m, :], PTk[g][cur][:, m, :])
                    for g in range(2):
                        Pp, PTq = bank(g, 0), bank(g, 1)
                        S.copy("act", Pk[g][nxt][:], Pp[:, :, :])
                        if k < 5:
                            S.copy("dve", PTk[g][nxt][:], PTq[:, :, :])
                    for g in range(2):
                        TTp = bank(g, 2)
                        for m in range(4):
                            S.mm(TTp[:, m, :], Pk[g][nxt][:, m, :], TTk[g][cur][:, m, :])
                    for g in range(2):
                        TTp = bank(g, 2)
                        S.tt("dve", TTk[g][nxt][:], TTp[:, :, :], TTk[g][cur][:], ALU.add)
                    cur = nxt
                for n in range(8):
                    d, c, h = n // 4, (n // 2) % 2, n % 2
                    kt = kv[d][s_][:, (c * 2 + h) * 2, :]
                    S.ts("dve", TTb[:, n, :], TTk[n // 4][cur][:, n % 4, :], btn[:, n:n + 1], None, ALU.mult)
                    S.ts("pool", keg[:, n, :], kt, eg[:, n:n + 1], None, ALU.mult)
                    S.ts("pool", kd[:, n, :], kt, edl[:, n:n + 1], None, ALU.mult)
                    S.tt("pool" if n % 2 else "dve", qgT[:, n, :], qk[d][s_][:, h, c * 64:(c + 1) * 64], egB[:, n, :], ALU.mult)
                for half in range(2):
                    up = ps[5][0:64, :].rearrange("p (a b) -> p a b", b=128)
                    for m in range(4):
                        n = half * 4 + m
                        d, c, h = n // 4, (n // 2) % 2, n % 2
                        S.mm(up[:, m, :], TTb[:, n, :], kv[d][s_][:, (c * 2 + h) * 2 + 1, :])
                    S.copy("act", u_sb[:, half * 4:half * 4 + 4, :], up[:, :, :])
                wp = ps[0][:, :].rearrange("p (a b) -> p a b", b=64)
                for n in range(8):
                    S.mm(wp[:, n, :], keg[:, n, :], TTb[:, n, :])
                S.copy("act", wT[:], wp[:, :, :])
                for cc_ in range(2):
                    for d in range(2):
                        c = cc_ if d == 0 else 1 - cc_
                        for h in range(2):
                            n = d * 4 + c * 2 + h
                            vn = vnew[h]
                            pv = ps[1][0:64, 0:128]
                            po = ps[1][0:64, 128:256]
                            pS = ps[2][:, 0:128]
                            S.mm(pv, wT[:, n, :], Sb[d][h][:])
                            S.tt("dve", vn[:], u_sb[:, n, :], pv, ALU.subtract)
                            if tl[d] >= 2:
                                S.mm(po, qgT[:, n, :], Sb[d][h][:], start=True, stop=False)
                                S.mm(po, intraT[:, n, :], vn[:], start=False, stop=True)
                                S.copy("act", o_sb[d][s_][:, c * 2 + h, :], po)
                            S.mm(pS, kd[:, n, :], vn[:])
                            S.stt("dve", Sf[d][h][:], Sf[d][h][:], egl[:, n:n + 1], pS, ALU.mult, ALU.add)
                            S.copy("act", Sb[d][h][:], Sf[d][h][:])
                for d in range(2):
                    if tl[d] >= 2:
                        S.dma("sp", OD[d, tl[d] - 2], o_sb[d][s_][:].rearrange("p a b -> p (a b)"))
            S.barrier()
            ofq = [[S.sb([64, 4, 128], F32, "ofq") for q in range(4)] for _ in range(2)]
            obq = [[S.sb([64, 4, 128], F32, "obq") for q in range(4)] for _ in range(2)]
            oacc = S.sb([64, 4, 128], F32, "oacc")
            zt = [S.sb([128, 2, 128], BF16, "zt") for _ in range(2)]
            ssq = S.sb([64, 4], F32, "ssq")
            junk3 = S.sb([64, 128], F32, "junk3")
            on = S.sb([64, 4, 128], BF16, "on")
            mx = [S.sb([128, 2, 128], BF16, "mx") for _ in range(2)]
            for s in range(NO):
                s_ = s % 2
                for q in range(4):
                    S.dma("sp", ofq[s_][q][:].rearrange("p a b -> p (a b)"), OD[0, s + NO * q])
                    S.dma("act", obq[s_][q][:].rearrange("p a b -> p (a b)"), OD[1, s + NO * q])
                S.dma("sp", zt[s_][:].rearrange("p a b -> p (a b)"), ZST[s, :, 0:256])
                for q in range(4):
                    S.tt("dve", ofq[s_][q][:], ofq[s_][q][:], obq[s_][q][:], ALU.add)
                    if q == 0:
                        S.ts("dve", oacc[:], ofq[s_][q][:], sel[:, 0:1], None, ALU.mult)
                    else:
                        S.stt("dve", oacc[:], ofq[s_][q][:], sel[:, q:q + 1], oacc[:], ALU.mult, ALU.add)
                for m in range(4):
                    S.act(junk3[:], oacc[:, m, :], AF.Square, accum_out=ssq[:, m:m + 1])
                S.act(ssq[:], ssq[:], AF.Sqrt, bias=epsc[0:64, 0:1], scale=1.0 / 128)
                S.op("dve", lambda e, o=ssq[:]: e.reciprocal(o, o), [ssq[:]], [ssq[:]])
                for m in range(4):
                    S.ts("dve" if m % 2 else "pool", on[:, m, :], oacc[:, m, :], ssq[:, m:m + 1], None, ALU.mult)
                oTp = ps[0][:, 0:256].rearrange("p (a b) -> p a b", b=128)
                for c in range(2):
                    for h in range(2):
                        S.mm(oTp[:, h, c * 64:(c + 1) * 64], on[:, c * 2 + h, :], identb64[:])
                S.stt("dve", mx[s_][:], oTp[:, :, :], gw[:, 0:1], zt[s_][:], ALU.mult, ALU.mult)
                S.dma("sp", MIXT[2 * j:2 * j + 2, :, s * 128:(s + 1) * 128].rearrange("h p n -> p h n"), mx[s_][:])
            S.phase_end()

            S.phase_begin()
            QW = 512 if LO >= 512 else LO
            NQ = LO // QW
            TPQ = QW // 128
            KT = S.sb([128, NK], BF16, "KT")
            V = S.sb([128, NT, 128], BF16, "V")
            S.dma("sp", KT[:], KAT)
            for t in range(NT):
                S.dma("act" if t % 2 else "sp", V[:, t, :], VA[t])
            qt = [S.sb([128, QW], BF16, "qt") for _ in range(2)]
            zt3 = [S.sb([128, QW], BF16, "zt3") for _ in range(2)]
            pt = [S.sb([128, QW], BF16, "pt") for _ in range(4)]
            rl = S.sb([128, QW], F32, "rl")
            ot = S.sb([128, QW], F32, "ot")
            mx3 = [S.sb([128, QW], BF16, "mx3") for _ in range(2)]
            sbank = [ps[0], ps[1], pb[0][:, :].bitcast(F32), pb[1][:, :].bitcast(F32)]
            it3 = 0
            for h in range(2):
                for qi in range(NQ):
                    s_ = it3 % 2
                    it3 += 1
                    S.dma("sp", qt[s_][:], QAT[h, :, qi * QW:(qi + 1) * QW])
                    for i in range(TPQ):
                        S.dma("act", zt3[s_][:, i * 128:(i + 1) * 128], ZST[qi * TPQ + i, :, 256 + h * 128:256 + (h + 1) * 128])
                    po = ps[2 + 2 * s_][:, 0:QW]
                    pl = ps[3 + 2 * s_][:, 0:QW]
                    for k0 in range(min(3, NT)):
                        S.mm(sbank[k0][:, 0:QW], KT[:, k0 * 128:(k0 + 1) * 128], qt[s_][:])
                    for kt in range(NT):
                        b_ = kt % 4
                        if kt + 3 < NT:
                            S.mm(sbank[(kt + 3) % 4][:, 0:QW], KT[:, (kt + 3) * 128:(kt + 4) * 128], qt[s_][:])
                        S.act(pt[b_][:], sbank[b_][:, 0:QW], AF.Exp)
                        S.mm(po, V[:, kt, :], pt[b_][:], start=(kt == 0), stop=(kt == NT - 1))
                        S.mm(pl, ones_b[:], pt[b_][:], start=(kt == 0), stop=(kt == NT - 1))
                    S.op("dve", lambda e, o=rl[:], i=pl: e.reciprocal(o, i), [rl[:]], [pl])
                    S.tt("dve", ot[:], po, rl[:], ALU.mult)
                    S.tt("pool", mx3[s_][:], ot[:], zt3[s_][:], ALU.mult)
                    S.dma("sp", MIXT[8 + 2 * j + h, :, qi * QW:(qi + 1) * QW], mx3[s_][:])
            S.phase_end()

        S.phase_begin()
        FW = S.sb([128, D], F32, "FW")
        S.dma("sp", FW[:], fnw)
        wo_sb = S.sb([128, NKC, D], BF16, "wosb")
        wf4 = [S.sb([128, 1024], F32, "wf4") for _ in range(2)]
        for kc in range(NKC):
            for hf in range(2):
                S.dma("sp", wf4[hf][:], wout[kc, :, hf * 1024:(hf + 1) * 1024])
                S.copy("pool" if hf else "dve", wo_sb[:, kc, hf * 1024:(hf + 1) * 1024], wf4[hf][:])
        mt = [S.sb([128, NKC, 128], BF16, "mt") for _ in range(2)]
        xt4 = [S.sb([128, D], F32, "xt4") for _ in range(2)]
        yt = S.sb([128, D], F32, "yt")
        junk4 = S.sb([128, D], BF16, "junk4")
        ss4 = S.sb([128, 1], F32, "ss4")
        ob4 = [S.sb([128, D], F32, "ob4") for _ in range(2)]
        for t in range(NO):
            s_ = t % 2
            S.dma("sp", mt[s_][:], MIXT[:, :, t * 128:(t + 1) * 128].rearrange("a p n -> p a n"))
            S.dma("act", xt4[s_][:], xo[t])
            for cb in range(4):
                for fc in range(NKC):
                    S.mm(ps[1 + cb][:, :], mt[s_][:, fc, :], wo_sb[:, fc, cb * 512:(cb + 1) * 512], start=(fc == 0), stop=(fc == NKC - 1))
                sl = slice(cb * 512, (cb + 1) * 512)
                S.tt("dve", yt[:, sl], ps[1 + cb][:, :], GT[:, sl], ALU.mult)
                S.tt("pool", yt[:, sl], yt[:, sl], xt4[s_][:, sl], ALU.add)
            S.act(junk4[:], yt[:], AF.Square, accum_out=ss4[:])
            S.act(ss4[:], ss4[:], AF.Sqrt, bias=epsc[:, 0:1], scale=1.0 / D)
            S.op("dve", lambda e, o=ss4[:]: e.reciprocal(o, o), [ss4[:]], [ss4[:]])
            S.stt("dve", ob4[s_][:], yt[:], ss4[:, 0:1], FW[:], ALU.mult, ALU.mult)
            S.dma("sp", out[t], ob4[s_][:])
        S.phase_end()
    return nc


def rope_tables(L):
    inv = (np.float32(10000.0) ** (-np.arange(32, dtype=np.float32) / np.float32(32))).astype(np.float32)
    tok = np.arange(L)
    row = (tok // 64).astype(np.float32)
    col = (tok % 64).astype(np.float32)
    ra = (row[:, None] * inv[None, :]).astype(np.float32)
    ca = (col[:, None] * inv[None, :]).astype(np.float32)
    rowcs = np.concatenate([np.cos(ra), np.sin(ra)], axis=1).astype(np.float32).reshape(L // 128, 128, 64)
    colcs = np.concatenate([np.cos(ca), np.sin(ca)], axis=1).astype(np.float32)[:128]
    return rowcs, colcs


def core_cols(j):
    hs = [2 * j, 2 * j + 1]
    kvh = j // 2
    cols = []
    for base in (0, 1024, 2048):
        for h in hs:
            cols += list(range(base + h * 128, base + (h + 1) * 128))
    for h in hs:
        cols += list(range(3072 + h * 128, 3072 + (h + 1) * 128))
    for h in hs:
        cols += list(range(5664 + h * 128, 5664 + (h + 1) * 128))
    for h in hs:
        cols += list(range(4128 + h * 128, 4128 + (h + 1) * 128))
    cols += list(range(5152 + kvh * 128, 5152 + (kvh + 1) * 128))
    cols += list(range(5408 + kvh * 128, 5408 + (kvh + 1) * 128))
    for ba in range(2):
        for d in range(2):
            for h in hs:
                cols.append(4096 + (ba * 2 + d) * 8 + h)
    return np.array(cols)


def p2_consts():
    i = np.arange(64)[:, None]
    j = np.arange(64)[None, :]
    tri0 = (i <= j).astype(np.float32)
    tri1 = (i >= j).astype(np.float32)
    m0 = np.where(i >= j, 0.0, BIG).astype(np.float32)
    m1 = np.where(i <= j, 0.0, BIG).astype(np.float32)
    ns0 = np.where(i > j, -1.0, 0.0).astype(np.float32)
    ns1 = np.where(i < j, -1.0, 0.0).astype(np.float32)
    ident = np.eye(64, dtype=np.float32)
    return np.ascontiguousarray(np.concatenate([tri0, tri1, m0, m1, ns0, ns1, ident], axis=1))


def prep_fused(inp, L):
    NO = L // 128 // 4
    rowcs, colcs = rope_tables(L)
    w_in = inp["w_in"][0]
    conv_w = inp["conv_w"][0]
    a_log = inp["a_log"][0]
    dt_bias = inp["dt_bias"][0]
    wcs, cws, als, dbs = [], [], [], []
    for j in range(4):
        cols = core_cols(j)
        wcs.append(w_in[:, cols].reshape(NKC, 128, NCOL))
        cw = conv_w[:, cols[:768]]
        cws.append(cw.T.reshape(6, 128, 5).transpose(1, 0, 2).reshape(128, 30))
        hs = [2 * j, 2 * j + 1]
        als.append(np.broadcast_to(np.array([a_log[d, h] for d in range(2) for h in hs], np.float32)[None, :], (64, 4)))
        dbs.append(np.broadcast_to(np.array([dt_bias[d, h] for d in range(2) for h in hs], np.float32)[None, :], (64, 4)))
    shared = {
        "dr_wada": np.ascontiguousarray(inp["w_ada"][0].reshape(NKC, 128, 6144)),
        "dr_bada": np.ascontiguousarray(np.broadcast_to(inp["b_ada"][0][None, :], (128, 6144))),
        "dr_normw": np.ascontiguousarray(np.broadcast_to(inp["norm_w"][0][None, :], (128, D))),
        "dr_wc": np.ascontiguousarray(np.stack(wcs)), "dr_convw": np.ascontiguousarray(np.stack(cws)),
        "dr_qnw": np.ascontiguousarray(np.broadcast_to(inp["q_norm_w"][0][None, :], (128, 128))),
        "dr_knw": np.ascontiguousarray(np.broadcast_to(inp["k_norm_w"][0][None, :], (128, 128))),
        "dr_rowcs": rowcs, "dr_colcs": colcs, "dr_ident": np.eye(128, dtype=np.float32),
        "dr_alog": np.ascontiguousarray(np.stack(als)), "dr_dtb": np.ascontiguousarray(np.stack(dbs)),
        "dr_gnw": np.ascontiguousarray(inp["gdn_norm_w"][0].reshape(128, 1)), "dr_cst": p2_consts(),
        "dr_wout": np.ascontiguousarray(inp["w_out"][0].reshape(NKC, 128, D)),
        "dr_fnw": np.ascontiguousarray(np.broadcast_to(inp["final_norm_w"][None, :], (128, D))),
    }
    maps = []
    for core in range(8):
        b, jq = core // 4, core % 4
        xh = np.concatenate([inp["ctx"][b], inp["x"][b]], axis=0).reshape(-1, 128, D)
        sel = np.zeros((64, 4), np.float32)
        sel[:, jq] = 1.0
        m = dict(shared)
        m.update({"dr_xh": np.ascontiguousarray(xh),
                  "dr_xo": np.ascontiguousarray(inp["x"][b, jq * NO * 128:(jq + 1) * NO * 128].reshape(NO, 128, D)),
                  "dr_cc": np.ascontiguousarray(np.concatenate([inp["c"][b].reshape(NKC, 128).T, inp["c_ctx"].reshape(NKC, 128).T], axis=1)),
                  "dr_rowcso": np.ascontiguousarray(rowcs[jq * NO:(jq + 1) * NO]), "dr_sel": sel})
        maps.append(m)
    return maps

from concourse.bass_utils import run_bass_kernel_spmd

_CACHE = {}


def kernel(x, c, ctx, c_ctx, w_ada, b_ada, norm_w, w_in, conv_w, a_log, dt_bias,
           gdn_norm_w, q_norm_w, k_norm_w, w_out, final_norm_w):
    inp = dict(x=np.asarray(x), c=np.asarray(c), ctx=np.asarray(ctx), c_ctx=np.asarray(c_ctx), w_ada=np.asarray(w_ada),
               b_ada=np.asarray(b_ada), norm_w=np.asarray(norm_w), w_in=np.asarray(w_in), conv_w=np.asarray(conv_w),
               a_log=np.asarray(a_log), dt_bias=np.asarray(dt_bias), gdn_norm_w=np.asarray(gdn_norm_w),
               q_norm_w=np.asarray(q_norm_w), k_norm_w=np.asarray(k_norm_w), w_out=np.asarray(w_out),
               final_norm_w=np.asarray(final_norm_w))
    B, L, _ = inp["x"].shape
    if L not in _CACHE:
        _CACHE[L] = build_fused(L)
    res = run_bass_kernel_spmd(_CACHE[L], prep_fused(inp, L), core_ids=list(range(8))).results
    NO = L // 128 // 4
    out = np.empty((B, L, 2048), np.float32)
    for core in range(8):
        b, jq = core // 4, core % 4
        out[b, jq * NO * 128:(jq + 1) * NO * 128] = np.asarray(res[core]["dr_out"]).reshape(NO * 128, 2048)
    return out
```
